# Optimizing a Trainium2 kernel written in Bass

```python
import jax, jax.numpy as jnp
from jax import lax
import numpy as np

D_MODEL = 1024
BATCH = 4
SEQ = 4096
DEPTH = 1

HGRN_HEADS = 4
HGRN_KEY_DIM = 128
HGRN_VAL_DIM = (D_MODEL // 2) // HGRN_HEADS
HGRN_KEY_WIDTH = HGRN_HEADS * HGRN_KEY_DIM
HGRN_WIDTH = HGRN_HEADS * HGRN_VAL_DIM
CHUNK = 64
POOL_WINDOWS = (2, 4, 8, 16)
POOL_GROUPS = len(POOL_WINDOWS)
POOL_WIDTH = D_MODEL - HGRN_WIDTH
POOL_GROUP_DIM = POOL_WIDTH // POOL_GROUPS
POOL_MAX_W = max(POOL_WINDOWS)
MIX_WIDTH = HGRN_WIDTH + POOL_WIDTH
IN_WIDTH = 2 * HGRN_KEY_WIDTH + 2 * HGRN_WIDTH + POOL_WIDTH
MEM_LEN = 256
XATTN_HEADS = 4
XATTN_HEAD_DIM = D_MODEL // XATTN_HEADS
D_FF = 4 * D_MODEL
EPS = 1e-6

kernel_name = "hymba_style_hgrn2_pool_hybrid"


def rmsnorm(x, g):
    xf = x.astype(jnp.float32)
    y = xf * lax.rsqrt(jnp.mean(xf * xf, axis=-1, keepdims=True) + EPS)
    return (y * g.astype(jnp.float32)).astype(x.dtype)


def hgrn2_chunkwise(q, k, v, log_f):
    B, S, H, DK = q.shape
    DV = v.shape[-1]
    n = S // CHUNK

    def to_chunks(a):
        return a.reshape(B, n, CHUNK, H, a.shape[-1]).transpose(1, 0, 3, 2, 4)

    qc, kc, vc, fc = to_chunks(q), to_chunks(k), to_chunks(v), to_chunks(log_f)
    causal = jnp.tril(jnp.ones((CHUNK, CHUNK), dtype=bool))

    def step(state, inp):
        q_, k_, v_, lf = inp
        b = jnp.cumsum(lf, axis=2)
        diff = b[:, :, :, None, :] - b[:, :, None, :, :]
        decay = jnp.exp(jnp.where(causal[None, None, :, :, None], diff, -jnp.inf))
        scores = jnp.einsum('bhtk,bhsk,bhtsk->bhts', q_, k_, decay)
        o_intra = jnp.einsum('bhts,bhsv->bhtv', scores, v_)
        o_inter = jnp.einsum('bhtk,bhkv->bhtv', q_ * jnp.exp(b), state)
        b_last = b[:, :, -1:, :]
        k_dec = k_ * jnp.exp(b_last - b)
        state = jnp.exp(b_last[:, :, 0, :])[..., None] * state + jnp.einsum('bhsk,bhsv->bhkv', k_dec, v_)
        return state, o_intra + o_inter

    s0 = jnp.zeros((B, H, DK, DV), jnp.float32)
    _, o = lax.scan(step, s0, (qc, kc, vc, fc))
    return o.transpose(1, 0, 3, 2, 4).reshape(B, S, H, DV)


def multiscale_pool(p, w_pool, pool_scale):
    B, S, _ = p.shape
    pg = p.astype(jnp.float32).reshape(B, S, POOL_GROUPS, POOL_GROUP_DIM)
    cs = jnp.cumsum(pg, axis=1)
    cs = jnp.pad(cs, ((0, 0), (POOL_MAX_W, 0), (0, 0), (0, 0)))
    pos = (jnp.arange(S) + 1)
    outs = []
    for gi, w in enumerate(POOL_WINDOWS):
        win = cs[:, POOL_MAX_W:, gi] - cs[:, POOL_MAX_W - w:POOL_MAX_W - w + S, gi]
        cnt = jnp.minimum(pos, w).astype(jnp.float32)[None, :, None]
        outs.append(win / cnt - pg[:, :, gi])
    pooled = jnp.stack(outs, axis=2)
    y = jnp.einsum('bsgc,gcd->bsgd', pooled, w_pool.astype(jnp.float32))
    return y.reshape(B, S, POOL_WIDTH) * pool_scale.astype(jnp.float32)


def setup_inputs(seed: int = 0) -> dict:
    key = jax.random.key(seed)
    ks = jax.random.split(key, 24)
    f32 = jnp.float32

    def dense(k, shape, fan_in):
        return jax.random.normal(k, shape, f32) * (fan_in ** -0.5)

    def gain(k, shape):
        return 1.0 + 0.02 * jax.random.normal(k, shape, f32)

    return {
        "x": jax.random.normal(ks[0], (BATCH, SEQ, D_MODEL), f32),
        "mem": jax.random.normal(ks[1], (BATCH, MEM_LEN, D_MODEL), f32),
        "norm_mix_g": gain(ks[2], (DEPTH, D_MODEL)),
        "w_in": dense(ks[3], (DEPTH, D_MODEL, IN_WIDTH), D_MODEL),
        "lb_logits": 0.5 * jax.random.normal(ks[4], (DEPTH + 1, HGRN_KEY_WIDTH), f32),
        "hgrn_norm_g": gain(ks[5], (DEPTH, HGRN_HEADS, HGRN_VAL_DIM)),
        "w_pool": dense(ks[6], (DEPTH, POOL_GROUPS, POOL_GROUP_DIM, POOL_GROUP_DIM), POOL_GROUP_DIM),
        "pool_scale": gain(ks[7], (DEPTH, POOL_WIDTH)),
        "w_out": dense(ks[8], (DEPTH, MIX_WIDTH, D_MODEL), MIX_WIDTH),
        "norm_x_g": gain(ks[9], (DEPTH, D_MODEL)),
        "norm_mem_g": gain(ks[10], (DEPTH, D_MODEL)),
        "w_xq": dense(ks[11], (DEPTH, D_MODEL, XATTN_HEADS, XATTN_HEAD_DIM), D_MODEL),
        "w_xk": dense(ks[12], (DEPTH, D_MODEL, XATTN_HEADS, XATTN_HEAD_DIM), D_MODEL),
        "w_xv": dense(ks[13], (DEPTH, D_MODEL, XATTN_HEADS, XATTN_HEAD_DIM), D_MODEL),
        "w_xo": dense(ks[14], (DEPTH, XATTN_HEADS, XATTN_HEAD_DIM, D_MODEL), D_MODEL),
        "norm_ffn_g": gain(ks[15], (DEPTH, D_MODEL)),
        "w_ff1": dense(ks[16], (DEPTH, D_MODEL, D_FF), D_MODEL),
        "w_ff2": dense(ks[17], (DEPTH, D_FF, D_MODEL), D_FF),
        "final_norm_g": gain(ks[18], (D_MODEL,)),
    }


def reference(x, mem, norm_mix_g, w_in, lb_logits, hgrn_norm_g, w_pool, pool_scale, w_out,
              norm_x_g, norm_mem_g, w_xq, w_xk, w_xv, w_xo, norm_ffn_g, w_ff1, w_ff2, final_norm_g):
    B, S, _ = x.shape
    f32 = jnp.float32
    lower_bounds = jnp.cumsum(jax.nn.softmax(lb_logits.astype(f32), axis=0), axis=0)
    split_at = [HGRN_KEY_WIDTH, 2 * HGRN_KEY_WIDTH,
                2 * HGRN_KEY_WIDTH + HGRN_WIDTH, 2 * HGRN_KEY_WIDTH + 2 * HGRN_WIDTH]
    for l in range(DEPTH):
        h = rmsnorm(x, norm_mix_g[l])
        z = jnp.einsum('bsd,de->bse', h, w_in[l])
        q_pre, f_pre, i_pre, g_pre, p = jnp.split(z, split_at, axis=-1)

        lb = lower_bounds[l]
        f = lb + (1.0 - lb) * jax.nn.sigmoid(f_pre.astype(f32))
        log_f = jnp.log(f)
        k = 1.0 - f
        q = jax.nn.silu(q_pre.astype(f32))
        hs = lambda a, d: a.reshape(B, S, HGRN_HEADS, d)
        o = hgrn2_chunkwise(hs(q, HGRN_KEY_DIM), hs(k, HGRN_KEY_DIM),
                            hs(i_pre.astype(f32), HGRN_VAL_DIM), hs(log_f, HGRN_KEY_DIM))
        o = o * lax.rsqrt(jnp.mean(o * o, axis=-1, keepdims=True) + EPS) * hgrn_norm_g[l].astype(f32)
        o_a = o.reshape(B, S, HGRN_WIDTH) * jax.nn.silu(g_pre.astype(f32))

        o_b = multiscale_pool(p, w_pool[l], pool_scale[l])

        mixed = jnp.concatenate([o_a, o_b], axis=-1).astype(x.dtype)
        x = x + jnp.einsum('bse,ed->bsd', mixed, w_out[l])

        hq = rmsnorm(x, norm_x_g[l])
        hm = rmsnorm(mem, norm_mem_g[l])
        xq = jnp.einsum('bsd,dhe->bshe', hq, w_xq[l])
        xk = jnp.einsum('bmd,dhe->bmhe', hm, w_xk[l])
        xv = jnp.einsum('bmd,dhe->bmhe', hm, w_xv[l])
        scores = jnp.einsum('bshe,bmhe->bhsm', xq, xk).astype(f32) * (XATTN_HEAD_DIM ** -0.5)
        probs = jax.nn.softmax(scores, axis=-1).astype(x.dtype)
        att = jnp.einsum('bhsm,bmhe->bshe', probs, xv)
        x = x + jnp.einsum('bshe,hed->bsd', att, w_xo[l])

        hf = rmsnorm(x, norm_ffn_g[l])
        u = jnp.square(jax.nn.relu(jnp.einsum('bsd,df->bsf', hf, w_ff1[l])))
        x = x + jnp.einsum('bsf,fd->bsd', u, w_ff2[l])
    return rmsnorm(x, final_norm_g)
```

```python
import contextlib
import math

import numpy as np

import concourse.bass as bass
import concourse.mybir as mybir
from concourse.bass_utils import run_bass_kernel_spmd

F32 = mybir.dt.float32
BF16 = mybir.dt.bfloat16
AF = mybir.ActivationFunctionType
ALU = mybir.AluOpType

ENGS = ("pe", "act", "dve", "pool", "sp")

D = 1024
NTOK = 2048
TILE = 512
EPS = 1e-6
POOL_W = (2, 4, 8, 16)


class Buf:
    __slots__ = ("name", "last_w", "readers")

    def __init__(self, name):
        self.name = name
        self.last_w = None
        self.readers = {}


class View:
    __slots__ = ("ap", "bufs")

    def __init__(self, ap, bufs):
        self.ap = ap
        self.bufs = list(bufs)


class Op:
    __slots__ = ("eng", "fn", "deps", "signal", "sig_count", "dma_sem", "name")

    def __init__(self, eng, fn, name=""):
        self.eng = eng
        self.fn = fn
        self.deps = []
        self.signal = False
        self.sig_count = None
        self.dma_sem = None
        self.name = name


def _flat(items):
    out = []
    for it in items:
        if it is None:
            continue
        if isinstance(it, Buf):
            out.append(it)
        elif isinstance(it, View):
            out.extend(it.bufs)
        else:
            out.extend(_flat(it))
    return out


class Sched:
    def __init__(self, nc):
        self.nc = nc
        self.eng_ops = {e: [] for e in ENGS}
        self.dma_counts = {}

    def _add_dep(self, op, prod):
        if prod is None or prod is op:
            return
        if prod.dma_sem is None:
            if prod.eng == op.eng and op.eng in ("pe", "sp"):
                return
            prod.signal = True
        op.deps.append(prod)

    def op(self, eng, fn, reads=(), writes=(), name=""):
        reads = _flat(reads)
        writes = _flat(writes)
        o = Op(eng, fn, name)
        for b in reads:
            self._add_dep(o, b.last_w)
        for b in writes:
            self._add_dep(o, b.last_w)
            for r in b.readers.values():
                self._add_dep(o, r)
        for b in reads:
            b.readers[eng] = o
        for b in writes:
            b.last_w = o
            b.readers = {}
        self.eng_ops[eng].append(o)
        return o

    def dma(self, queue, out, in_, semkey, reads=(), writes=(), name=""):
        reads = _flat(reads)
        writes = _flat(writes)

        def fn(e):
            return e.dma_start(out=out, in_=in_)

        o = Op(queue, fn, name)
        o.dma_sem = semkey
        n = self.dma_counts.get(semkey, 0) + 1
        self.dma_counts[semkey] = n
        o.sig_count = 16 * n
        o.signal = True
        for b in reads:
            self._add_dep(o, b.last_w)
        for b in writes:
            self._add_dep(o, b.last_w)
            for r in b.readers.values():
                self._add_dep(o, r)
        for b in reads:
            b.readers[("dma", semkey, n)] = o
        for b in writes:
            b.last_w = o
            b.readers = {}
        self.eng_ops[queue].append(o)
        return o

    def emit(self, final_wait_sems=()):
        nc = self.nc
        for e in ENGS:
            c = 0
            for o in self.eng_ops[e]:
                if o.dma_sem is None and o.signal:
                    c += 1
                    o.sig_count = c
        sems = {}
        for e in ENGS:
            sems[("eng", e)] = nc.alloc_semaphore(name=f"sem_{e}")
        for k in self.dma_counts:
            sems[("dma", k)] = nc.alloc_semaphore(name=f"dsem_{k}")
        stats = {}
        with nc.Block() as block:
            def run(eng_name, handle):
                waited = {}
                nwait = 0
                for o in self.eng_ops[eng_name]:
                    need = {}
                    for p in o.deps:
                        ch = ("dma", p.dma_sem) if p.dma_sem is not None else ("eng", p.eng)
                        v = p.sig_count
                        if v > need.get(ch, 0):
                            need[ch] = v
                    for ch, v in need.items():
                        if waited.get(ch, 0) >= v:
                            continue
                        handle.wait_ge(sems[ch], v)
                        waited[ch] = v
                        nwait += 1
                    ins = o.fn(handle)
                    if o.dma_sem is not None:
                        ins.then_inc(sems[("dma", o.dma_sem)], 16)
                    elif o.signal:
                        ins.then_inc(sems[("eng", eng_name)], 1)
                if eng_name == "sp":
                    for k in final_wait_sems:
                        handle.wait_ge(sems[("dma", k)], 16 * self.dma_counts[k])
                stats[eng_name] = (len(self.eng_ops[eng_name]), nwait)

            @block.tensor
            def _(t):
                run("pe", t)

            @block.scalar
            def _(s):
                run("act", s)

            @block.vector
            def _(v):
                run("dve", v)

            @block.gpsimd
            def _(g):
                run("pool", g)

            @block.sync
            def _(sp):
                run("sp", sp)
        return stats


class Arena:
    GRAN = 1024

    def __init__(self, nc, es, name, nbytes):
        self.nbytes = nbytes
        self.t = es.enter_context(nc.sbuf_tensor(name, [128, nbytes // 4], F32))
        self.bufs = [Buf(f"{name}_g{i}") for i in range((nbytes + self.GRAN - 1) // self.GRAN)]
        self.cur = 0

    def view(self, off, nbytes, dtype=F32, pattern=None, **kw):
        assert off % 4 == 0 and nbytes % 4 == 0 and off + nbytes <= self.nbytes, (off, nbytes, self.nbytes)
        ap = self.t[:, off // 4:(off + nbytes) // 4]
        if dtype == BF16:
            ap = ap.bitcast(BF16)
        if pattern is not None:
            ap = ap.rearrange(pattern, **kw)
        g0 = off // self.GRAN
        g1 = (off + nbytes + self.GRAN - 1) // self.GRAN
        return View(ap, self.bufs[g0:g1])

    def alloc(self, nbytes, dtype=F32, pattern=None, **kw):
        off = (self.cur + self.GRAN - 1) // self.GRAN * self.GRAN
        self.cur = off + nbytes
        assert self.cur <= self.nbytes, ("arena overflow", self.cur, self.nbytes)
        return self.view(off, nbytes, dtype, pattern, **kw)


class _Stop(Exception):
    pass


def build_program(stop_after=None):
    nc = bass.Bass("TRN2", target_bir_lowering=False)
    S = Sched(nc)

    def ckpt(name):
        if stop_after is not None and name == stop_after:
            raise _Stop()

    def din(name, shape):
        return nc.dram_tensor(name, list(shape), F32, kind="ExternalInput").ap()

    x_d = din("x", [NTOK, D])
    xp_d = din("xp", [NTOK, D])
    mem_d = din("mem", [256, D])
    w_in_d = din("w_in", [D, 2560])
    w_out_d = din("w_out", [D, D])
    w_xq_d = din("w_xq", [D, D])
    w_xk_d = din("w_xk", [D, D])
    w_xv_d = din("w_xv", [D, D])
    w_xo_d = din("w_xo", [D, D])
    w_ff1_d = din("w_ff1", [D, 4096])
    w_ff2_d = din("w_ff2", [4096, D])
    w_pool_d = din("w_pool", [4, 128, 128])
    gains_d = din("gains", [5, D])
    lbl_d = din("lbl", [128, 8])
    gno_d = din("gno", [128, 4])
    psc_d = din("psc", [128, 4])
    icnt_d = din("icnt", [128, 64])
    cst_d = din("cst", [128, 128 + 512 + 512])
    y_d = nc.dram_tensor("y", [NTOK, D], F32, kind="ExternalOutput").ap()

    es = contextlib.ExitStack()
    with es:
        def sb(name, shape, dt):
            return es.enter_context(nc.sbuf_tensor("sb_" + name, list(shape), dt))

        xres = sb("xres", [128, 8, D], F32)
        xbuf = [Buf(f"x{i}") for i in range(8)]
        NSLOT = 7
        wslots = sb("wslots", [128, NSLOT, 8, 512], BF16)
        wslot_buf = [Buf(f"ws{i}") for i in range(NSLOT)]
        wpool = sb("wpool", [128, 4, 128], BF16)
        wpool_b = Buf("wpool")
        gain_t = [sb(f"gain{i}", [128, D], F32) for i in range(2)]
        gain_b = [Buf(f"gain{i}") for i in range(2)]
        scanmask_t = sb("scanmask", [128, 512], BF16)
        ident = sb("ident", [128, 128], BF16)
        ones = sb("ones", [128, 128], BF16)
        ones512 = sb("ones512", [128, 512], BF16)
        mask4 = sb("mask4", [128, 512], BF16)
        cb_b = Buf("constb")
        small = sb("small", [128, 32], F32)
        small_b = Buf("small")
        lbl = sb("lbl", [128, 8], F32)
        gno = sb("gno", [128, 4], F32)
        psc = sb("psc", [128, 4], F32)
        icnt = sb("icnt", [128, 4, 16], F32)
        Sst = sb("Sst", [128, 2, 4, 128], F32)
        S_b = [[Buf(f"S{pp}{h}") for h in range(4)] for pp in range(2)]
        s_cur = [0, 0, 0, 0]
        smid = sb("smid", [128, 4, 2, 128], BF16)
        smid_b = [[Buf(f"smid{h}{j}") for j in range(2)] for h in range(4)]
        stat = sb("stat", [128, 64], F32)
        stat_b = [Buf(f"stat{i}") for i in range(8)]
        mhalf = sb("mhalf", [128, 8], F32)
        hvec = sb("hvec", [128, 4, 48], F32)
        hvec_b = [Buf(f"hvec{h}") for h in range(4)]
        halo = sb("halo", [128, 4, 16], F32)
        halo_b = Buf("halo")
        kT_t = sb("kT", [128, 8, 256], BF16)
        vmem_t = sb("vmem", [128, 2, D], BF16)
        kT = View(kT_t[:], [Buf("kT")])
        vmem = View(vmem_t[:], [Buf("vmem")])

        SCR = 91 * 1024
        ar = Arena(nc, es, "scr", SCR)

        banks_t = [es.enter_context(nc.psum_tensor(f"bank{i}", [128, 512], F32)) for i in range(8)]

        class Bank:
            def __init__(self, i):
                self.f = banks_t[i][:]
                self.bf = banks_t[i][:].bitcast(BF16)
                self.buf = Buf(f"bank{i}")

        banks = [Bank(i) for i in range(8)]
        bank_ctr = [0]

        def next_bank():
            b = banks[bank_ctr[0] % 8]
            bank_ctr[0] += 1
            return b

        hT = [ar.alloc(8192, BF16, "p (k t) -> p k t", k=8) for _ in range(2)]
        HB0 = ar.cur
        hb = [ar.alloc(2048, BF16) for _ in range(4)]
        HB1 = ar.cur
        ar.cur = HB0
        qe = [ar.alloc(1024, BF16) for _ in range(4)]
        sT = [ar.alloc(1024, BF16) for _ in range(4)]
        ke = [ar.alloc(1024, BF16) for _ in range(2)]
        kd = [ar.alloc(1024, BF16) for _ in range(2)]
        kdT = [ar.alloc(1024, BF16, "p (n k) -> p n k", n=4) for _ in range(2)]
        assert ar.cur >= HB1
        junk = hb[3]
        PH = ar.cur
        vtok = ar.alloc(4096, BF16, "p (n c) -> p n c", n=4)
        _gate_blk = ar.alloc(4096, BF16)
        gate = [View(_gate_blk.ap[:, 512 * h_:512 * h_ + 512], _gate_blk.bufs[h_:h_ + 1]) for h_ in range(4)]
        vtok2 = View(_gate_blk.ap.rearrange("p (n c) -> p n c", n=4), _gate_blk.bufs)
        mixedT = ar.alloc(8192, BF16, "p (k t) -> p k t", k=8)
        pbuf = [ar.alloc(4 * 528, F32) for g in range(4)]
        kq_k = [ar.alloc(2048, F32) for _ in range(4)]
        kq_q = [ar.alloc(2048, F32) for _ in range(4)]
        tC = [ar.alloc(4 * 528, F32) for _ in range(2)]
        tD = [ar.alloc(4 * 528, F32) for _ in range(2)]
        sgt = View(tC[1].ap[:, 0:512], tC[1].bufs)
        pooled = [ar.alloc(1024, BF16) for _ in range(4)]
        P1_END = ar.cur
        ar.cur = PH
        memst = [ar.alloc(4096, F32) for _ in range(2)]
        xqT = ar.alloc(8192, BF16, "p (j t) -> p j t", j=8)
        pT = [ar.alloc(2048, BF16, "p (m t) -> p m t", m=2) for _ in range(4)]
        attT = ar.alloc(8192, BF16, "p (j t) -> p j t", j=8)
        rden = [ar.alloc(2048, F32) for _ in range(2)]
        ar.cur = PH
        uT = [ar.alloc(8192, BF16, "p (j t) -> p j t", j=8) for _ in range(2)]
        rtmp = [ar.alloc(2048, F32) for _ in range(3)]
        cst_stage = ar.view(PH + 40 * 1024, 4 * (128 + 512), F32)

        def wsrc_rowsplit(w, c0):
            return w.rearrange("(k p) n -> p k n", p=128)[:, :, c0:c0 + 512]

        units = {}
        for i, nm in enumerate(["in_q", "in_f", "in_i", "in_g", "in_p"]):
            units[nm] = wsrc_rowsplit(w_in_d, 512 * i)
        for nm, w in (("out", w_out_d), ("xq", w_xq_d), ("xk", w_xk_d), ("xv", w_xv_d), ("xo", w_xo_d)):
            for hf in range(2):
                units[f"{nm}{hf}"] = wsrc_rowsplit(w, 512 * hf)
        for j in range(8):
            units[f"ff1_{j}"] = wsrc_rowsplit(w_ff1_d, 512 * j)
        w2v = w_ff2_d.rearrange("(q f p) n -> q p f n", f=8, p=128)
        for q in range(4):
            for hf in range(2):
                units[f"ff2_{q}{hf}"] = w2v[q][:, :, 512 * hf:512 * hf + 512]

        def st_seq(st):
            s = []
            if st == 1:
                s += ["in_f", "in_i", "in_p"]
            s += ["in_q", "in_g", "out0", "out1", "xq0", "xq1", "xo0", "xo1"]
            for q in range(4):
                s += [f"ff1_{2 * q}", f"ff1_{2 * q + 1}", f"ff2_{q}0", f"ff2_{q}1"]
            return [(u, st) for u in s]

        load_seq = [("in_f", 0), ("in_i", 0), ("in_p", 0), ("xk0", 0), ("xk1", 0), ("xv0", 0), ("xv1", 0)]
        load_seq += st_seq(0) + st_seq(1)
        load_pos = [0]
        free_slots = list(range(NSLOT))
        loaded = {}

        def pump(startup=False):
            while free_slots and load_pos[0] < len(load_seq):
                key = load_seq[load_pos[0]]
                idx = load_pos[0]
                load_pos[0] += 1
                sl = free_slots.pop(0)
                loaded[key] = sl
                extra = []
                if startup and idx >= 2:
                    extra = xbuf[0:4] if idx < 4 else xbuf[0:8]
                S.dma("pool", wslots[:, sl], units[key[0]], f"w{sl}", reads=extra, writes=[wslot_buf[sl]],
                      name=f"ld_{key}")

        def W(name, st):
            key = (name, st)
            assert key in loaded, f"weight unit {key} not loaded yet"
            sl = loaded[key]
            return wslots[:, sl], wslot_buf[sl]

        def release(name, st):
            sl = loaded.pop((name, st))
            free_slots.append(sl)
            pump()

        for i_ in range(4):
            S.dma("sp", xres[:, i_, :], xp_d[128 * i_:128 * i_ + 128, :], f"x{i_}", writes=[xbuf[i_]])
        S.dma("sp", cst_stage.ap, cst_d[:, 0:640], "c0", writes=[cst_stage])
        S.dma("pool", scanmask_t[:], cst_d[:, 640:1152], "c0b", writes=[cb_b])
        vec_b = [Buf(f"vec{i}") for i in range(4)]
        S.dma("sp", lbl[:], lbl_d[:, :], "c1", writes=[vec_b[0]])
        S.dma("sp", gno[:], gno_d[:, :], "c2", writes=[vec_b[1]])
        S.dma("sp", psc[:], psc_d[:, :], "c3", writes=[vec_b[2]])
        S.dma("sp", icnt[:].rearrange("p g t -> p (g t)"), icnt_d[:, :], "c4", writes=[vec_b[3]])
        S.dma("pool", wpool[:], w_pool_d.rearrange("g c d -> c g d"), "c5", writes=[wpool_b])

        def load_gain(slot, row):
            S.dma("sp", gain_t[slot][:], gains_d[row:row + 1, :].partition_broadcast(128), f"g{slot}",
                  writes=[gain_b[slot]])

        G_MIX, G_X, G_MEM, G_FFN, G_FIN = 0, 1, 2, 3, 4
        load_gain(0, G_MIX)
        load_gain(1, G_MEM)

        S.op("dve", lambda e: e.tensor_copy(ident[:], cst_stage.ap[:, 0:128]), reads=[cst_stage], writes=[cb_b])
        S.op("dve", lambda e: e.tensor_copy(mask4[:], cst_stage.ap[:, 128:640]), reads=[cst_stage], writes=[cb_b])
        S.op("pool", lambda e: e.memset(ones[:], 1.0), writes=[cb_b])
        S.op("pool", lambda e: e.memset(ones512[:], 1.0), writes=[cb_b])
        S.op("pool", lambda e: e.memset(mhalf[:], -0.5), writes=[small_b])
        S.op("pool", lambda e: e.memset(Sst[:], 0.0), writes=S_b)
        S.op("dve", lambda e: e.tensor_tensor(small[:, 16:20], lbl[:, 0:4], lbl[:, 4:8], ALU.subtract),
             reads=[vec_b], writes=[small_b])
        S.op("act", lambda e: e.activation(out=small[:, 0:4], in_=small[:, 16:20], func=AF.Sigmoid),
             reads=[small_b], writes=[small_b])
        S.op("dve", lambda e: e.tensor_scalar(small[:, 4:8], small[:, 0:4], -1.0, 1.0, ALU.mult, ALU.add),
             reads=[small_b], writes=[small_b])
        S.op("dve", lambda e: e.tensor_scalar(small[:, 8:12], small[:, 0:4], 1.0, -1.0, ALU.mult, ALU.add),
             reads=[small_b], writes=[small_b])
        oml_c = lambda h: small[:, 4 + h:5 + h]
        noml_c = lambda h: small[:, 8 + h:9 + h]

        stat_ctr = [0]
        evac_ctr = [0]

        def evac_copy(dst_ap, src_ap, reads, writes, eng=None):
            if eng is None:
                eng = "act" if evac_ctr[0] % 2 == 0 else "dve"
                evac_ctr[0] += 1
            if eng == "act":
                S.op("act", lambda e: e.copy(dst_ap, src_ap), reads=reads, writes=writes)
            else:
                S.op("dve", lambda e: e.tensor_copy(dst_ap, src_ap), reads=reads, writes=writes)

        def norm_stats(src_blocks):
            si = stat_ctr[0] % 8
            stat_ctr[0] += 1
            st_ap = stat[:, 8 * si:8 * si + 8]
            stb = stat_b[si]
            nb = len(src_blocks)
            for i, (xap, xb) in enumerate(src_blocks):
                S.op("act", lambda e, xap=xap, i=i: e.activation(out=hb[i].ap, in_=xap, func=AF.Square,
                                                               accum_out=st_ap[:, i:i + 1]),
                     reads=[xb], writes=[hb[i], stb])
            S.op("act", lambda e: e.activation(out=st_ap[:, 0:nb], in_=st_ap[:, 0:nb], func=AF.Ln, scale=1.0 / D, bias=EPS),
                 reads=[stb], writes=[stb])
            S.op("act", lambda e: e.activation(out=st_ap[:, 4:4 + nb], in_=st_ap[:, 0:nb], func=AF.Exp, scale=-0.5),
                 reads=[stb], writes=[stb])
            return (lambda i: st_ap[:, 4 + i:5 + i]), stb

        def norm_elem(src_blocks, gslot):
            rs, stb = norm_stats(src_blocks)
            for i, (xap, xb) in enumerate(src_blocks):
                S.op("dve", lambda e, xap=xap, i=i: e.scalar_tensor_tensor(
                    hb[i].ap, xap, rs(i), gain_t[gslot][:], ALU.mult, ALU.mult),
                    reads=[xb, stb, gain_b[gslot]], writes=[hb[i]])

        def norm_T(nb, hT_dst):
            w = nb * 128
            for kp in range(4):
                bk = next_bank()
                for kk in range(2):
                    kc = 2 * kp + kk
                    for i in range(nb):
                        S.op("pe", lambda e, bk=bk, kk=kk, i=i, kc=kc: e.transpose(
                            bk.bf[:, kk * 512 + i * 128: kk * 512 + (i + 1) * 128],
                            hb[i].ap[:, kc * 128:(kc + 1) * 128], ident[:]),
                            reads=[hb[i], cb_b], writes=[bk.buf])
                for kk in range(2):
                    kc = 2 * kp + kk
                    evac_copy(hT_dst.ap[:, kc, 0:w], bk.bf[:, kk * 512:kk * 512 + w], [], [bk.buf, hT_dst.bufs[kc]],
                              eng=("act" if kk == 0 else "dve"))

        def norm_tile(src_blocks, gslot, hT_dst):
            norm_elem(src_blocks, gslot)
            norm_T(len(src_blocks), hT_dst)

        class Hooks:
            def __init__(self, a=None, b=None):
                self.a = a
                self.b = b

            def run_a(self):
                if self.a is not None:
                    self.a()
                    self.a = None

            def run_b(self):
                if self.b is not None:
                    self.b()
                    self.b = None

        def norm_hooks(src_blocks, gslot, hT_dst):
            return Hooks(lambda: norm_elem(src_blocks, gslot), lambda: norm_T(len(src_blocks), hT_dst))

        NOHOOK = Hooks()

        def proj_B(unit_ap, unit_buf, mchunk, hT_src, ncols=512, col0=0):
            bk = next_bank()
            for kc in range(8):
                S.op("pe", lambda e, bk=bk, kc=kc: e.matmul(
                    bk.f[:, 0:ncols], unit_ap[:, kc, 128 * mchunk:128 * mchunk + 128],
                    hT_src.ap[:, kc, col0:col0 + ncols], start=(kc == 0), stop=(kc == 7)),
                    reads=[unit_buf, hT_src.bufs[kc]], writes=[bk.buf])
            return bk

        def proj_A(lhs_view, blk, unit_ap, unit_buf, order=(0, 1, 2, 3, 4, 5, 6, 7)):
            bk = next_bank()
            for idx, j in enumerate(order):
                S.op("pe", lambda e, bk=bk, j=j, idx=idx: e.matmul(
                    bk.f[:, :], lhs_view.ap[:, j, 128 * blk:128 * blk + 128], unit_ap[:, j, :],
                    start=(idx == 0), stop=(idx == 7)),
                    reads=[unit_buf, lhs_view.bufs[j]], writes=[bk.buf])
            return bk

        def resid_add(bk, blk, hf):
            xa = xres[:, blk, 512 * hf:512 * hf + 512]
            S.op("dve", lambda e: e.tensor_tensor(xa, xa, bk.f[:, :], ALU.add),
                 reads=[xbuf[blk]], writes=[xbuf[blk], bk.buf])

        def load_x_block(src_d, row0, blk):
            S.dma("sp", xres[:, blk, :], src_d[row0:row0 + 128, :], f"x{blk}", writes=[xbuf[blk]])

        class HC:
            pass

        def hg_ln(c):
            S.op("act", lambda e: e.activation(out=c.Cv, in_=kq_k[c.h].ap, func=AF.Ln, scale=-1.0, bias=1.0),
                 reads=[kq_k[c.h]], writes=[c.tC])

        def hg_scan(c):
            S.op("dve", lambda e: e.tensor_tensor_scan(out=c.Dv, data0=scanmask_t[:], data1=c.Cv, initial=0.0,
                                                       op0=ALU.mult, op1=ALU.add),
                 reads=[c.tC, cb_b], writes=[c.tD])

        def hg_extract(c, main):
            hv, d3 = c.hv, c.d3
            if main:
                S.op("dve", lambda e: e.tensor_copy(hv[:, 0:16].rearrange("p (k c) -> p c k", k=2), d3[:, :, 31::32]),
                     reads=[c.tD], writes=[c.hvb])
                S.op("dve", lambda e: e.tensor_tensor(hv[:, 16:24], hv[:, 8:16], hv[:, 0:8], ALU.subtract),
                     reads=[c.hvb], writes=[c.hvb])
            else:
                S.op("dve", lambda e: e.tensor_copy(hv[:, 8:16], d3[:, :, 63]), reads=[c.tD], writes=[c.hvb])
                S.op("dve", lambda e: e.tensor_copy(hv[:, 0:8], d3[:, :, 63]), reads=[c.tD], writes=[c.hvb])

        def hg_exp_small(c, main):
            hv = c.hv
            if main:
                S.op("act", lambda e: e.activation(out=hv[:, 24:48], in_=hv[:, 0:24], func=AF.Exp),
                     reads=[c.hvb], writes=[c.hvb])
            else:
                S.op("act", lambda e: e.activation(out=hv[:, 8:16], in_=hv[:, 8:16], func=AF.Exp),
                     reads=[c.hvb], writes=[c.hvb])

        def hg_sub(c, main):
            col = 0
            S.op("dve", lambda e: e.tensor_tensor(
                c.d3, c.d3, c.hv[:, col:col + 8].rearrange("p (c o) -> p c o", o=1).to_broadcast([128, 8, 64]),
                ALU.subtract), reads=[c.tD, c.hvb], writes=[c.tD])

        def hg_exp_big(c, main):
            if main:
                S.op("act", lambda e: e.activation(out=c.Cv, in_=c.Dv, func=AF.Exp), reads=[c.tD], writes=[c.tC])
            S.op("act", lambda e: e.activation(out=c.Dv, in_=c.Dv, func=AF.Exp, scale=-1.0), reads=[c.tD], writes=[c.tD])

        def hg_mults(c, main):
            h, r = c.h, c.r
            if main:
                S.op("dve", lambda e: e.tensor_tensor(qe[h].ap, kq_q[h].ap, c.Cv, ALU.mult),
                     reads=[kq_q[h], c.tC], writes=[qe[h]])
                S.op("dve", lambda e: e.tensor_tensor(ke[r].ap, kq_k[h].ap, c.Dv, ALU.mult),
                     reads=[kq_k[h], c.tD], writes=[ke[r]])
                ke3 = ke[r].ap.rearrange("p (c j) -> p c j", j=64)
                kd3 = kd[r].ap.rearrange("p (c j) -> p c j", j=64)
                S.op("pool", lambda e: e.tensor_tensor(
                    kd3, ke3, c.hv[:, 40:48].rearrange("p (c o) -> p c o", o=1).to_broadcast([128, 8, 64]), ALU.mult),
                    reads=[ke[r], c.hvb], writes=[kd[r]])
            else:
                S.op("dve", lambda e: e.tensor_tensor(kd[r].ap, kq_k[h].ap, c.Dv, ALU.mult),
                     reads=[kq_k[h], c.tD], writes=[kd[r]])

        def hg_pe_front(c, main):
            h, r = c.h, c.r
            if main:
                sb_ = next_bank()
                for n in range(4):
                    S.op("pe", lambda e, n=n: e.matmul(
                        sb_.f[:, 128 * n:128 * n + 128], ke[r].ap[:, 128 * n:128 * n + 128],
                        qe[h].ap[:, 128 * n:128 * n + 128], start=True, stop=True),
                        reads=[ke[r], qe[h]], writes=[sb_.buf])
                S.op("dve", lambda e: e.tensor_tensor(sT[h].ap, sb_.f[:, :], mask4[:], ALU.mult),
                     reads=[cb_b], writes=[sb_.buf, sT[h]])
            bk = next_bank()
            for n in range(4):
                S.op("pe", lambda e, n=n: e.transpose(bk.bf[:, 128 * n:128 * n + 128],
                                                      kd[r].ap[:, 128 * n:128 * n + 128], ident[:]),
                     reads=[kd[r], cb_b], writes=[bk.buf])
            evac_copy(kdT[r].ap.rearrange("p n k -> p (n k)"), bk.bf[:, 0:512], [], [bk.buf, kdT[r]], eng="act")

        def hg_pe_state(c):
            h, r = c.h, c.r
            c.pbanks = [next_bank(), next_bank()]
            for ch in range(8):
                n, half = divmod(ch, 2)
                pb = c.pbanks[half]
                S.op("pe", lambda e, pb=pb, n=n, half=half: e.matmul(
                    pb.f[:, n * 128:n * 128 + 128],
                    kdT[r].ap[64 * half:64 * half + 64, n, :],
                    vtok.ap[64 * half:64 * half + 64, n, 128 * h:128 * h + 128], start=True, stop=True),
                    reads=[kdT[r], vtok.bufs[n]], writes=[pb.buf])

        def hg_state_step(c, ch):
            h = c.h
            cur = s_cur[h]
            nxt = 1 - cur
            pb = c.pbanks[ch % 2]
            S.op("dve", lambda e: e.scalar_tensor_tensor(
                Sst[:, nxt, h, :], Sst[:, cur, h, :], c.hv[:, 32 + ch:33 + ch], pb.f[:, (ch // 2) * 128:(ch // 2) * 128 + 128],
                ALU.mult, ALU.add),
                reads=[S_b[cur][h], c.hvb], writes=[S_b[nxt][h], pb.buf])
            s_cur[h] = nxt

        def hg_smid(c, ch):
            h = c.h
            cur = s_cur[h]
            j = ch % 2
            S.op("pool", lambda e: e.tensor_scalar(smid[:, h, j, :], Sst[:, cur, h, :], c.hv[:, 24 + ch:25 + ch], 0.0,
                                                   ALU.mult, ALU.add),
                 reads=[S_b[cur][h], c.hvb], writes=[smid_b[h][j]])

        def pf_front_stages(cs):
            def st_ln():
                for c in cs:
                    hg_ln(c)

            def st_scan():
                for c in cs:
                    S.op("dve", lambda e, c=c: e.tensor_tensor_scan(out=c.Dv, data0=ones512[:], data1=c.Cv, initial=0.0,
                                                                op0=ALU.mult, op1=ALU.add),
                         reads=[c.tC, cb_b], writes=[c.tD])

            def st_extract():
                for c in cs:
                    S.op("dve", lambda e, c=c: e.tensor_copy(c.hv[:, 0:1], c.Dv[:, 511:512]), reads=[c.tD], writes=[c.hvb])

            def st_exp_small():
                for c in cs:
                    S.op("act", lambda e, c=c: e.activation(out=c.hv[:, 8:9], in_=c.hv[:, 0:1], func=AF.Exp),
                         reads=[c.hvb], writes=[c.hvb])

            def st_sub():
                for c in cs:
                    S.op("dve", lambda e, c=c: e.tensor_scalar(c.Dv, c.Dv, c.hv[:, 0:1], None, ALU.subtract),
                         reads=[c.tD, c.hvb], writes=[c.tD])

            def st_exp_big():
                for c in cs:
                    hg_exp_big(c, False)

            def st_mults():
                for c in cs:
                    hg_mults(c, False)
            return [st_ln, st_scan, st_extract, st_exp_small, st_sub, st_exp_big, st_mults]

        def pf_pe(cs, vt=None):
            vt = vtok if vt is None else vt
            for c in cs:
                hg_pe_front(c, False)
            for c in cs:
                h, r = c.h, c.r
                c.pb = next_bank()
                for n in range(4):
                    S.op("pe", lambda e, c=c, n=n, h=h, r=r: e.matmul(
                        c.pb.f[:, 0:128], kdT[r].ap[:, n, :], vt.ap[:, n, 128 * h:128 * h + 128],
                        start=(n == 0), stop=(n == 3)),
                        reads=[kdT[r], vt.bufs[n]], writes=[c.pb.buf])

        def pf_step(cs):
            for c in cs:
                h = c.h
                cur = s_cur[h]
                nxt = 1 - cur
                S.op("dve", lambda e, c=c, h=h, cur=cur, nxt=nxt: e.scalar_tensor_tensor(
                    Sst[:, nxt, h, :], Sst[:, cur, h, :], c.hv[:, 8:9], c.pb.f[:, 0:128], ALU.mult, ALU.add),
                    reads=[S_b[cur][h], c.hvb], writes=[S_b[nxt][h], c.pb.buf])
                s_cur[h] = nxt

        def make_hc(h):
            c = HC()
            c.h = h
            c.r = h % 2
            c.hv = hvec[:, h, :]
            c.hvb = hvec_b[h]
            c.tC = tC[c.r]
            c.tD = tD[c.r]
            c.Cv = tC[c.r].ap[:, 0:512]
            c.Dv = tD[c.r].ap[:, 0:512]
            c.d3 = c.Dv.rearrange("p (c j) -> p c j", j=64)
            return c

        out_sems = set()
        try:
            def prefix_norm(q):
                blks = [4 * (q % 2) + i for i in range(4)]
                norm_tile([(xres[:, b, :], xbuf[b]) for b in blks], 0, hT[q % 2])

            for i in range(4):
                load_x_block(xp_d, 512 + 128 * i, 4 + i)
            pump(startup=True)
            prefix_norm(0)
            for q in range(4):
                hTq = hT[q % 2]
                if q + 1 < 4:
                    nblks = [4 * ((q + 1) % 2) + i for i in range(4)]
                    norm_elem([(xres[:, b, :], xbuf[b]) for b in nblks], 0)
                for i in range(4):
                    blk = 4 * (q % 2) + i
                    if q + 2 < 4:
                        load_x_block(xp_d, 512 * (q + 2) + 128 * i, blk)
                    else:
                        load_x_block(x_d, 128 * blk, blk)
                uf, ufb = W("in_f", 0)
                ui, uib = W("in_i", 0)
                for h in range(4):
                    bk = proj_B(uf, ufb, h, hTq)
                    S.op("act", lambda e, bk=bk, h=h: e.activation(out=kq_k[h].ap, in_=bk.f[:, :], func=AF.Sigmoid),
                         writes=[bk.buf, kq_k[h]])
                    S.op("dve", lambda e, h=h: e.tensor_scalar(kq_k[h].ap, kq_k[h].ap, noml_c(h), oml_c(h), ALU.mult, ALU.add),
                         reads=[kq_k[h], small_b], writes=[kq_k[h]])
                cs0 = [make_hc(0), make_hc(1)]
                cs1 = [make_hc(2), make_hc(3)]
                PF0 = pf_front_stages(cs0)
                PF0[0]()
                PF0[1]()
                vt_q = vtok if q % 2 == 0 else vtok2
                if q == 0:
                    for n in range(4):
                        bk = proj_A(hTq, n, ui, uib)
                        evac_copy(vt_q.ap[:, n, :], bk.f[:, :], [], [bk.buf, vt_q.bufs[n]])
                if q + 1 < 4:
                    norm_T(4, hT[(q + 1) % 2])
                if q == 3:
                    up, upb = W("in_p", 0)
                    for g in range(4):
                        bk = proj_B(up, upb, g, hTq, ncols=16, col0=496)
                        S.op("act", lambda e, bk=bk, g=g: e.copy(halo[:, g, :], bk.f[:, 0:16]),
                             writes=[bk.buf, halo_b])
                for th in PF0[2:]:
                    th()
                pf_pe(cs0, vt_q)
                PF1 = pf_front_stages(cs1)
                PF1[0]()
                PF1[1]()
                if q + 1 < 4:
                    vt_n = vtok if (q + 1) % 2 == 0 else vtok2
                    for n in range(4):
                        bk = proj_A(hT[(q + 1) % 2], n, ui, uib)
                        evac_copy(vt_n.ap[:, n, :], bk.f[:, :], [], [bk.buf, vt_n.bufs[n]])
                for th in PF1[2:]:
                    th()
                pf_step(cs0)
                pf_pe(cs1, vt_q)
                pf_step(cs1)
                ckpt(f"prefix{q}")

            for mb in range(2):
                S.dma("sp", memst[mb].ap, mem_d[128 * mb:128 * mb + 128, :], f"m{mb}", writes=[memst[mb]])
            norm_tile([(memst[mb].ap, memst[mb]) for mb in range(2)], 1, hT[0])
            hmT = hT[0]
            for uh in range(2):
                uk, ukb = W(f"xk{uh}", 0)
                for jp in range(2):
                    bk = next_bank()
                    for jj in range(2):
                        j4 = 2 * jp + jj
                        for kc in range(8):
                            S.op("pe", lambda e, bk=bk, kc=kc, jj=jj, j4=j4, uk=uk: e.matmul(
                                bk.f[:, 256 * jj:256 * jj + 256], uk[:, kc, 128 * j4:128 * j4 + 128], hmT.ap[:, kc, 0:256],
                                start=(kc == 0), stop=(kc == 7)),
                                reads=[ukb, hmT], writes=[bk.buf])
                    j0 = 4 * uh + 2 * jp
                    evac_copy(kT.ap[:, j0:j0 + 2, :].rearrange("p j m -> p (j m)"), bk.f[:, :], [], [bk.buf, kT])
            for mb in range(2):
                for hf in range(2):
                    uv, uvb = W(f"xv{hf}", 0)
                    bk = proj_A(hmT, mb, uv, uvb)
                    evac_copy(vmem.ap[:, mb, 512 * hf:512 * hf + 512], bk.f[:, :], [], [bk.buf, vmem])
            for nm in ("xk0", "xk1", "xv0", "xv1"):
                release(nm, 0)
            load_gain(1, G_X)
            ckpt("kv")


            def mixer_tile(st, t, hk=NOHOOK, late=False):
                hTt = hT[t]
                blks = [4 * t + i for i in range(4)]
                uq, uqb = W("in_q", st)
                uf, ufb = W("in_f", st)
                ui, uib = W("in_i", st)
                ug, ugb = W("in_g", st)
                up, upb = W("in_p", st)
                for g in range(4):
                    bk = proj_B(up, upb, g, hTt)
                    S.op("act", lambda e, bk=bk, g=g: e.copy(pbuf[g].ap[:, 16:528], bk.f[:, :]),
                         writes=[bk.buf, pbuf[g]])
                    S.op("pool", lambda e, g=g: e.tensor_copy(pbuf[g].ap[:, 0:16], halo[:, g, :]),
                         reads=[halo_b], writes=[pbuf[g]])
                for g in range(4):
                    wlen = POOL_W[g]
                    cur = pbuf[g].ap
                    curv = pbuf[g]
                    sh = 1
                    tmps = [tC[0], tD[0]]
                    ti = 0
                    while sh < wlen:
                        dst = tmps[ti % 2]
                        lo = 2 * sh - 1
                        S.op("pool", lambda e, dst=dst, cur=cur, sh=sh, lo=lo: e.tensor_tensor(
                            dst.ap[:, lo:528], cur[:, lo:528], cur[:, lo - sh:528 - sh], ALU.add),
                            reads=[curv], writes=[dst])
                        cur = dst.ap
                        curv = dst
                        ti += 1
                        sh *= 2
                    if t == 0 and st == 0:
                        S.op("pool", lambda e, g=g, cur=cur: e.tensor_tensor(
                            cur[:, 16:32], cur[:, 16:32], icnt[:, g, :], ALU.mult),
                            reads=[curv, vec_b], writes=[curv])
                        S.op("pool", lambda e, g=g, cur=cur, wlen=wlen: e.tensor_scalar(
                            cur[:, 32:528], cur[:, 32:528], 1.0 / wlen, 0.0, ALU.mult, ALU.add),
                            reads=[curv], writes=[curv])
                    else:
                        S.op("pool", lambda e, g=g, cur=cur, wlen=wlen: e.tensor_scalar(
                            cur[:, 16:528], cur[:, 16:528], 1.0 / wlen, 0.0, ALU.mult, ALU.add),
                            reads=[curv], writes=[curv])
                    S.op("pool", lambda e, g=g, cur=cur: e.tensor_tensor(
                        pooled[g].ap, cur[:, 16:528], pbuf[g].ap[:, 16:528], ALU.subtract),
                        reads=[curv, pbuf[g]], writes=[pooled[g]])
                    S.op("pool", lambda e, g=g: e.tensor_copy(halo[:, g, :], pbuf[g].ap[:, 512:528]),
                         reads=[pbuf[g]], writes=[halo_b])
                if not late:
                    hk.run_a()
                for h in range(4):
                    bk = proj_B(uf, ufb, h, hTt)
                    S.op("act", lambda e, bk=bk, h=h: e.activation(out=kq_k[h].ap, in_=bk.f[:, :], func=AF.Sigmoid),
                         writes=[bk.buf, kq_k[h]])
                    S.op("dve", lambda e, h=h: e.tensor_scalar(kq_k[h].ap, kq_k[h].ap, noml_c(h), oml_c(h), ALU.mult, ALU.add),
                         reads=[kq_k[h], small_b], writes=[kq_k[h]])
                for h in range(4):
                    bk = proj_B(uq, uqb, h, hTt)
                    S.op("act", lambda e, bk=bk: e.activation(out=sgt.ap, in_=bk.f[:, :], func=AF.Sigmoid),
                         writes=[bk.buf, sgt])
                    S.op("dve", lambda e, bk=bk, h=h: e.tensor_tensor(kq_q[h].ap, sgt.ap, bk.f[:, :], ALU.mult),
                         reads=[sgt], writes=[bk.buf, kq_q[h]])
                if late:
                    hk.run_a()
                else:
                    hk.run_b()
                def do_poolmm():
                    for g in range(4):
                        bk = next_bank()
                        S.op("pe", lambda e, bk=bk, g=g: e.matmul(bk.f[:, :], wpool[:, g, :], pooled[g].ap, start=True, stop=True),
                             reads=[wpool_b, pooled[g]], writes=[bk.buf])
                        S.op("act", lambda e, bk=bk, g=g: e.activation(out=mixedT.ap[:, 4 + g, :], in_=bk.f[:, :], func=AF.Copy,
                                                                     scale=psc[:, g:g + 1]),
                             reads=[vec_b], writes=[bk.buf, mixedT.bufs[4 + g]])
                def do_g():
                    for h in range(4):
                        bk = proj_B(ug, ugb, h, hTt)
                        S.op("act", lambda e, bk=bk: e.activation(out=sgt.ap, in_=bk.f[:, :], func=AF.Sigmoid),
                             writes=[bk.buf, sgt])
                        S.op("dve", lambda e, bk=bk, h=h: e.tensor_tensor(gate[h].ap, sgt.ap, bk.f[:, :], ALU.mult),
                             reads=[sgt], writes=[bk.buf, gate[h]])
                def do_v():
                    for n in range(4):
                        bk = proj_A(hTt, n, ui, uib)
                        evac_copy(vtok.ap[:, n, :], bk.f[:, :], [], [bk.buf, vtok.bufs[n]])
                def front_stages(cs):
                    L = []
                    for stage in (hg_ln, hg_scan):
                        L.append(lambda stage=stage: [stage(c) for c in cs])
                    for stage in (hg_extract, hg_exp_small, hg_sub, hg_exp_big, hg_mults):
                        L.append(lambda stage=stage: [stage(c, True) for c in cs])
                    return L

                def pe_part(cs):
                    for c in cs:
                        hg_pe_front(c, True)
                    for c in cs:
                        hg_pe_state(c)

                def chain_blocks(cs):
                    for c in cs:
                        c.ob = next_bank()

                    def blk(n):
                        for c in cs:
                            h, r = c.h, c.r
                            S.op("pe", lambda e, c=c, n=n, h=h, r=r: e.matmul(
                                c.ob.f[:, 128 * n:128 * n + 128], vtok.ap[:, n, 128 * h:128 * h + 128],
                                sT[h].ap[:, 128 * n:128 * n + 128], start=True, stop=False),
                                reads=[vtok.bufs[n], sT[h]], writes=[c.ob.buf])
                        for j in range(2):
                            ch = 2 * n + j
                            for c in cs:
                                h, r = c.h, c.r
                                hg_smid(c, ch)
                                S.op("pe", lambda e, c=c, n=n, j=j, h=h, r=r: e.matmul(
                                    c.ob.f[:, 128 * n + 64 * j:128 * n + 64 * j + 64], smid[:, h, j, :],
                                    qe[h].ap[:, 128 * n + 64 * j:128 * n + 64 * j + 64], start=False, stop=(j == 1)),
                                    reads=[smid_b[h][j], qe[h]], writes=[c.ob.buf])
                                hg_state_step(c, ch)
                    return [lambda n=n: blk(n) for n in range(4)]

                def onorm_steps(cs):
                    def s1():
                        for c in cs:
                            S.op("act", lambda e, c=c: e.activation(out=sT[c.h].ap, in_=c.ob.f[:, :], func=AF.Square),
                                 writes=[c.ob.buf, sT[c.h]])
                        for c in cs:
                            c.nb = next_bank()
                            S.op("pe", lambda e, c=c: e.matmul(c.nb.f[:, :], ones[:], sT[c.h].ap, start=True, stop=True),
                                 reads=[cb_b, sT[c.h]], writes=[c.nb.buf])

                    def s2():
                        for c in cs:
                            S.op("act", lambda e, c=c: e.activation(out=kq_k[c.h].ap, in_=c.nb.f[:, :], func=AF.Ln, scale=1.0 / 128, bias=EPS),
                                 writes=[c.nb.buf, kq_k[c.h]])
                        for c in cs:
                            S.op("act", lambda e, c=c: e.activation(out=kq_k[c.h].ap, in_=kq_k[c.h].ap, func=AF.Exp, scale=-0.5),
                                 reads=[kq_k[c.h]], writes=[kq_k[c.h]])

                    def s3():
                        for c in cs:
                            S.op("dve", lambda e, c=c: e.tensor_tensor(kq_q[c.h].ap, kq_k[c.h].ap, c.ob.f[:, :], ALU.mult),
                                 reads=[kq_k[c.h]], writes=[c.ob.buf, kq_q[c.h]])
                        for c in cs:
                            S.op("dve", lambda e, c=c: e.scalar_tensor_tensor(
                                mixedT.ap[:, c.h, :], kq_q[c.h].ap, gno[:, c.h:c.h + 1], gate[c.h].ap, ALU.mult, ALU.mult),
                                reads=[kq_q[c.h], vec_b, gate[c.h]], writes=[mixedT.bufs[c.h]])
                    return [s1, s2, s3]

                def weave(A, B):
                    A = list(A)
                    B = list(B)
                    while A or B:
                        if A:
                            A.pop(0)()
                        if B:
                            B.pop(0)()

                cs0 = [make_hc(0), make_hc(1)]
                cs1 = [make_hc(2), make_hc(3)]
                F0 = front_stages(cs0)
                do_g()
                F0[0]()
                F0[1]()
                do_v()
                hk.run_b()
                for th_ in F0[2:]:
                    th_()
                do_poolmm()
                pe_part(cs0)
                weave(front_stages(cs1), chain_blocks(cs0) + onorm_steps(cs0))
                pe_part(cs1)
                for th in chain_blocks(cs1) + onorm_steps(cs1):
                    th()

            def mixer_out(st, t):
                for n in range(4):
                    for hf in range(2):
                        uo, uob = W(f"out{hf}", st)
                        bk = proj_A(mixedT, n, uo, uob, order=(4, 5, 6, 7, 0, 1, 2, 3))
                        resid_add(bk, 4 * t + n, hf)

            def xattn_tile(st, t, hk=NOHOOK):
                hTt = hT[t]
                hk.run_a()
                for j in range(8):
                    uq, uqb = W(f"xq{j // 4}", st)
                    bk = proj_B(uq, uqb, j % 4, hTt)
                    evac_copy(xqT.ap[:, j, :], bk.f[:, :], [], [bk.buf, xqT.bufs[j]])
                hk.run_b()
                def scores_exp(h):
                    for mc in range(2):
                        bk = next_bank()
                        for ec in range(2):
                            j = 2 * h + ec
                            S.op("pe", lambda e, bk=bk, mc=mc, j=j, ec=ec: e.matmul(
                                bk.f[:, :], kT.ap[:, j, 128 * mc:128 * mc + 128], xqT.ap[:, j, :],
                                start=(ec == 0), stop=(ec == 1)),
                                reads=[kT, xqT.bufs[j]], writes=[bk.buf])
                        S.op("act", lambda e, bk=bk, h=h, mc=mc: e.activation(out=pT[h].ap[:, mc, :], in_=bk.f[:, :],
                                                                           func=AF.Exp, scale=1.0 / 16),
                             writes=[bk.buf, pT[h].bufs[mc]])

                def den_att(h):
                    db = next_bank()
                    for mc in range(2):
                        S.op("pe", lambda e, db=db, h=h, mc=mc: e.matmul(db.f[:, :], ones[:], pT[h].ap[:, mc, :],
                                                                      start=(mc == 0), stop=(mc == 1)),
                             reads=[cb_b, pT[h].bufs[mc]], writes=[db.buf])
                    rd = rden[h % 2]
                    S.op("act", lambda e, db=db, rd=rd: e.activation(out=rd.ap, in_=db.f[:, :], func=AF.Ln),
                         writes=[db.buf, rd])
                    S.op("act", lambda e, rd=rd: e.activation(out=rd.ap, in_=rd.ap, func=AF.Exp, scale=-1.0),
                         reads=[rd], writes=[rd])
                    for ec in range(2):
                        j = 2 * h + ec
                        bk = next_bank()
                        for mc in range(2):
                            S.op("pe", lambda e, bk=bk, mc=mc, j=j, h=h: e.matmul(
                                bk.f[:, :], vmem.ap[:, mc, 128 * j:128 * j + 128], pT[h].ap[:, mc, :],
                                start=(mc == 0), stop=(mc == 1)),
                                reads=[vmem, pT[h].bufs[mc]], writes=[bk.buf])
                        S.op("dve", lambda e, bk=bk, j=j, rd=rd: e.tensor_tensor(attT.ap[:, j, :], rd.ap, bk.f[:, :], ALU.mult),
                             reads=[rd], writes=[bk.buf, attT.bufs[j]])

                scores_exp(0)
                scores_exp(1)
                for h in range(4):
                    if h + 2 < 4:
                        scores_exp(h + 2)
                    den_att(h)
                for n in range(4):
                    for hf in range(2):
                        uo, uob = W(f"xo{hf}", st)
                        bk = proj_A(attT, n, uo, uob)
                        resid_add(bk, 4 * t + n, hf)

            ffn_ctr = [0]

            def ffn_ff1(st, q, t, hk=NOHOOK):
                hTt = hT[t]
                hk.run_a()
                u = uT[ffn_ctr[0] % 2]
                ffn_ctr[0] += 1
                for fc in range(8):
                    u1, u1b = W(f"ff1_{2 * q + fc // 4}", st)
                    bk = proj_B(u1, u1b, fc % 4, hTt)
                    rt = rtmp[fc % 3]
                    S.op("act", lambda e, bk=bk, rt=rt: e.activation(out=rt.ap, in_=bk.f[:, :], func=AF.Relu),
                         writes=[bk.buf, rt])
                    S.op("pool", lambda e, rt=rt, fc=fc, u=u: e.tensor_tensor(u.ap[:, fc, :], rt.ap, rt.ap, ALU.mult),
                         reads=[rt], writes=[u.bufs[fc]])
                    if fc == 6:
                        hk.run_b()
                return u

            def ffn_ff2(st, q, t, u, hk=NOHOOK):
                hk.run_a()
                for n in range(4):
                    for hf in range(2):
                        u2, u2b = W(f"ff2_{q}{hf}", st)
                        bk = proj_A(u, n, u2, u2b)
                        resid_add(bk, 4 * t + n, hf)
                    if n == 1:
                        hk.run_b()

            def final_tile(st, t, gslot):
                blks = [4 * t + i for i in range(4)]
                rs, stb = norm_stats([(xres[:, b, :], xbuf[b]) for b in blks])
                for i, b in enumerate(blks):
                    S.op("dve", lambda e, b=b, i=i: e.scalar_tensor_tensor(
                        xres[:, b, :], xres[:, b, :], rs(i), gain_t[gslot][:], ALU.mult, ALU.mult),
                        reads=[xbuf[b], stb, gain_b[gslot]], writes=[xbuf[b]])
                    row0 = 1024 * st + 128 * b
                    S.dma("sp", y_d[row0:row0 + 128, :], xres[:, b, :], f"x{b}", reads=[xbuf[b]])
                    out_sems.add(f"x{b}")

            def x_blocks(t):
                return [(xres[:, 4 * t + i, :], xbuf[4 * t + i]) for i in range(4)]

            for st in range(2):
                if st == 0:
                    norm_tile(x_blocks(0), 0, hT[0])
                mixer_tile(st, 0, norm_hooks(x_blocks(1), 0, hT[1]), late=(st == 1))
                mixer_out(st, 0)
                mixer_tile(st, 1, norm_hooks(x_blocks(0), 1, hT[0]))
                for nm in ("in_q", "in_f", "in_i", "in_g", "in_p"):
                    release(nm, st)
                mixer_out(st, 1)
                release("out0", st)
                release("out1", st)
                load_gain(0, G_FFN)
                ckpt(f"p1_{st}")
                xattn_tile(st, 0, norm_hooks(x_blocks(1), 1, hT[1]))
                xattn_tile(st, 1, norm_hooks(x_blocks(0), 0, hT[0]))
                for nm in ("xq0", "xq1", "xo0", "xo1"):
                    release(nm, st)
                load_gain(1, G_FIN)
                ckpt(f"p2_{st}")
                items = [(q, t) for q in range(4) for t in range(2)]
                us = {items[0]: ffn_ff1(st, 0, 0, norm_hooks(x_blocks(1), 0, hT[1]))}
                for i, (q, t) in enumerate(items):
                    if i + 1 < len(items):
                        q2, t2 = items[i + 1]
                        us[(q2, t2)] = ffn_ff1(st, q2, t2)
                    hk = NOHOOK
                    if (q, t) == (3, 1) and st == 0:
                        load_gain(0, G_MIX)
                        hk = norm_hooks(x_blocks(0), 0, hT[0])
                    ffn_ff2(st, q, t, us[(q, t)], hk)
                    if q == 3:
                        final_tile(st, t, 1)
                        if st == 0:
                            for i_ in range(4):
                                load_x_block(x_d, 1024 + 128 * (4 * t + i_), 4 * t + i_)
                    if t == 1:
                        for nm in (f"ff1_{2 * q}", f"ff1_{2 * q + 1}", f"ff2_{q}0", f"ff2_{q}1"):
                            release(nm, st)
                        ckpt(f"f{q}_{st}")
                if st == 0:
                    load_gain(1, G_X)


        except _Stop:
            for b_ in range(8):
                S.dma("sp", y_d[128 * b_:128 * b_ + 128, :], xres[:, b_, :], f"x{b_}", reads=[xbuf[b_]])
                out_sems.add(f"x{b_}")
        stats = S.emit(final_wait_sems=sorted(out_sems))
        build_program.stats = stats
    return nc


_CACHE = {}


def _constants():
    ident = np.eye(128, dtype=np.float32)
    s = np.arange(128)[:, None]
    t = np.arange(128)[None, :]
    m = ((t >= s) & ((s // 64) == (t // 64))).astype(np.float32)
    mask4 = np.tile(m, (1, 4))
    scan = np.ones((128, 512), np.float32)
    scan[:, ::64] = 0.0
    return np.ascontiguousarray(np.concatenate([ident, mask4, scan], axis=1))


def kernel(x, mem, norm_mix_g, w_in, lb_logits, hgrn_norm_g, w_pool, pool_scale, w_out,
           norm_x_g, norm_mem_g, w_xq, w_xk, w_xv, w_xo, norm_ffn_g, w_ff1, w_ff2, final_norm_g):
    f32 = np.float32
    x = np.asarray(x, f32)
    mem = np.asarray(mem, f32)
    if "nc" not in _CACHE:
        _CACHE["nc"] = build_program()
    nc = _CACHE["nc"]
    c = lambda a: np.ascontiguousarray(np.asarray(a, f32))
    gains = c(np.stack([np.asarray(norm_mix_g)[0], np.asarray(norm_x_g)[0], np.asarray(norm_mem_g)[0],
                        np.asarray(norm_ffn_g)[0], np.asarray(final_norm_g)]))
    lbl = c(np.asarray(lb_logits, f32).reshape(2, 4, 128).transpose(2, 0, 1).reshape(128, 8))
    gno = c(np.asarray(hgrn_norm_g, f32)[0].T)
    psc = c(np.asarray(pool_scale, f32)[0].reshape(4, 128).T)
    shared = {
        "w_in": c(np.asarray(w_in)[0]), "w_out": c(np.asarray(w_out)[0]),
        "w_xq": c(np.asarray(w_xq)[0].reshape(D, D)), "w_xk": c(np.asarray(w_xk)[0].reshape(D, D)),
        "w_xv": c(np.asarray(w_xv)[0].reshape(D, D)), "w_xo": c(np.asarray(w_xo)[0].reshape(D, D)),
        "w_ff1": c(np.asarray(w_ff1)[0]), "w_ff2": c(np.asarray(w_ff2)[0]),
        "w_pool": c(np.asarray(w_pool)[0]), "gains": gains, "lbl": lbl, "gno": gno, "psc": psc,
        "cst": _constants(),
    }
    icnt_first = np.zeros((4, 16), f32)
    icnt_rest = np.zeros((4, 16), f32)
    for g, w in enumerate(POOL_W):
        for t in range(16):
            icnt_first[g, t] = 1.0 / min(t + 1, w)
            icnt_rest[g, t] = 1.0 / w
    in_maps = []
    for core in range(8):
        b, half = divmod(core, 2)
        m = dict(shared)
        m["x"] = c(x[b, half * NTOK:(half + 1) * NTOK])
        m["xp"] = c(x[b, 0:NTOK]) if half == 1 else np.zeros((NTOK, D), f32)
        m["mem"] = c(mem[b])
        ic = icnt_first if half == 0 else icnt_rest
        m["icnt"] = c(np.broadcast_to(ic.reshape(1, 64), (128, 64)))
        in_maps.append(m)
    res = run_bass_kernel_spmd(nc, in_maps, core_ids=list(range(8)))
    out = np.empty((4, 4096, D), f32)
    for core in range(8):
        b, half = divmod(core, 2)
        out[b, half * NTOK:(half + 1) * NTOK] = res.results[core]["y"]
    return out
```

```python
import contextlib
import math

import numpy as np

import concourse.bass as bass
import concourse.mybir as mybir
from concourse.bass_utils import run_bass_kernel_spmd

F32 = mybir.dt.float32
BF16 = mybir.dt.bfloat16
AF = mybir.ActivationFunctionType
ALU = mybir.AluOpType

ENGS = ("pe", "act", "dve", "pool", "sp")

D = 1024
NTOK = 2048
TILE = 512
EPS = 1e-6
POOL_W = (2, 4, 8, 16)


class Buf:
    __slots__ = ("name", "last_w", "readers")

    def __init__(self, name):
        self.name = name
        self.last_w = None
        self.readers = {}


class View:
    __slots__ = ("ap", "bufs")

    def __init__(self, ap, bufs):
        self.ap = ap
        self.bufs = list(bufs)


class Op:
    __slots__ = ("eng", "fn", "deps", "signal", "sig_count", "dma_sem", "name")

    def __init__(self, eng, fn, name=""):
        self.eng = eng
        self.fn = fn
        self.deps = []
        self.signal = False
        self.sig_count = None
        self.dma_sem = None
        self.name = name


def _flat(items):
    out = []
    for it in items:
        if it is None:
            continue
        if isinstance(it, Buf):
            out.append(it)
        elif isinstance(it, View):
            out.extend(it.bufs)
        else:
            out.extend(_flat(it))
    return out


class Sched:
    def __init__(self, nc):
        self.nc = nc
        self.eng_ops = {e: [] for e in ENGS}
        self.dma_counts = {}

    def _add_dep(self, op, prod):
        if prod is None or prod is op:
            return
        if prod.dma_sem is None:
            if prod.eng == op.eng and op.eng in ("pe", "sp"):
                return
            prod.signal = True
        op.deps.append(prod)

    def op(self, eng, fn, reads=(), writes=(), name=""):
        reads = _flat(reads)
        writes = _flat(writes)
        o = Op(eng, fn, name)
        for b in reads:
            self._add_dep(o, b.last_w)
        for b in writes:
            self._add_dep(o, b.last_w)
            for r in b.readers.values():
                self._add_dep(o, r)
        for b in reads:
            b.readers[eng] = o
        for b in writes:
            b.last_w = o
            b.readers = {}
        self.eng_ops[eng].append(o)
        return o

    def dma(self, queue, out, in_, semkey, reads=(), writes=(), name=""):
        reads = _flat(reads)
        writes = _flat(writes)

        def fn(e):
            return e.dma_start(out=out, in_=in_)

        o = Op(queue, fn, name)
        o.dma_sem = semkey
        n = self.dma_counts.get(semkey, 0) + 1
        self.dma_counts[semkey] = n
        o.sig_count = 16 * n
        o.signal = True
        for b in reads:
            self._add_dep(o, b.last_w)
        for b in writes:
            self._add_dep(o, b.last_w)
            for r in b.readers.values():
                self._add_dep(o, r)
        for b in reads:
            b.readers[("dma", semkey, n)] = o
        for b in writes:
            b.last_w = o
            b.readers = {}
        self.eng_ops[queue].append(o)
        return o

    def emit(self, final_wait_sems=()):
        nc = self.nc
        for e in ENGS:
            c = 0
            for o in self.eng_ops[e]:
                if o.dma_sem is None and o.signal:
                    c += 1
                    o.sig_count = c
        sems = {}
        for e in ENGS:
            sems[("eng", e)] = nc.alloc_semaphore(name=f"sem_{e}")
        for k in self.dma_counts:
            sems[("dma", k)] = nc.alloc_semaphore(name=f"dsem_{k}")
        stats = {}
        with nc.Block() as block:
            def run(eng_name, handle):
                waited = {}
                nwait = 0
                for o in self.eng_ops[eng_name]:
                    need = {}
                    for p in o.deps:
                        ch = ("dma", p.dma_sem) if p.dma_sem is not None else ("eng", p.eng)
                        v = p.sig_count
                        if v > need.get(ch, 0):
                            need[ch] = v
                    for ch, v in need.items():
                        if waited.get(ch, 0) >= v:
                            continue
                        handle.wait_ge(sems[ch], v)
                        waited[ch] = v
                        nwait += 1
                    ins = o.fn(handle)
                    if o.dma_sem is not None:
                        ins.then_inc(sems[("dma", o.dma_sem)], 16)
                    elif o.signal:
                        ins.then_inc(sems[("eng", eng_name)], 1)
                if eng_name == "sp":
                    for k in final_wait_sems:
                        handle.wait_ge(sems[("dma", k)], 16 * self.dma_counts[k])
                stats[eng_name] = (len(self.eng_ops[eng_name]), nwait)

            @block.tensor
            def _(t):
                run("pe", t)

            @block.scalar
            def _(s):
                run("act", s)

            @block.vector
            def _(v):
                run("dve", v)

            @block.gpsimd
            def _(g):
                run("pool", g)

            @block.sync
            def _(sp):
                run("sp", sp)
        return stats


class Arena:
    GRAN = 1024

    def __init__(self, nc, es, name, nbytes):
        self.nbytes = nbytes
        self.t = es.enter_context(nc.sbuf_tensor(name, [128, nbytes // 4], F32))
        self.bufs = [Buf(f"{name}_g{i}") for i in range((nbytes + self.GRAN - 1) // self.GRAN)]
        self.cur = 0

    def view(self, off, nbytes, dtype=F32, pattern=None, **kw):
        assert off % 4 == 0 and nbytes % 4 == 0 and off + nbytes <= self.nbytes, (off, nbytes, self.nbytes)
        ap = self.t[:, off // 4:(off + nbytes) // 4]
        if dtype == BF16:
            ap = ap.bitcast(BF16)
        if pattern is not None:
            ap = ap.rearrange(pattern, **kw)
        g0 = off // self.GRAN
        g1 = (off + nbytes + self.GRAN - 1) // self.GRAN
        return View(ap, self.bufs[g0:g1])

    def alloc(self, nbytes, dtype=F32, pattern=None, **kw):
        off = (self.cur + self.GRAN - 1) // self.GRAN * self.GRAN
        self.cur = off + nbytes
        assert self.cur <= self.nbytes, ("arena overflow", self.cur, self.nbytes)
        return self.view(off, nbytes, dtype, pattern, **kw)


class _Stop(Exception):
    pass


def build_program(stop_after=None):
    nc = bass.Bass("TRN2", target_bir_lowering=False)
    S = Sched(nc)

    def ckpt(name):
        if stop_after is not None and name == stop_after:
            raise _Stop()

    def din(name, shape):
        return nc.dram_tensor(name, list(shape), F32, kind="ExternalInput").ap()

    x_d = din("x", [NTOK, D])
    xp_d = din("xp", [NTOK, D])
    mem_d = din("mem", [256, D])
    w_in_d = din("w_in", [D, 2560])
    w_out_d = din("w_out", [D, D])
    w_xq_d = din("w_xq", [D, D])
    w_xk_d = din("w_xk", [D, D])
    w_xv_d = din("w_xv", [D, D])
    w_xo_d = din("w_xo", [D, D])
    w_ff1_d = din("w_ff1", [D, 4096])
    w_ff2_d = din("w_ff2", [4096, D])
    w_pool_d = din("w_pool", [4, 128, 128])
    gains_d = din("gains", [5, D])
    lbl_d = din("lbl", [128, 8])
    gno_d = din("gno", [128, 4])
    psc_d = din("psc", [128, 4])
    icnt_d = din("icnt", [128, 64])
    cst_d = din("cst", [128, 128 + 512 + 512])
    y_d = nc.dram_tensor("y", [NTOK, D], F32, kind="ExternalOutput").ap()

    es = contextlib.ExitStack()
    with es:
        def sb(name, shape, dt):
            return es.enter_context(nc.sbuf_tensor("sb_" + name, list(shape), dt))

        xres = sb("xres", [128, 8, D], F32)
        xbuf = [Buf(f"x{i}") for i in range(8)]
        NSLOT = 7
        wslots = sb("wslots", [128, NSLOT, 8, 512], BF16)
        wslot_buf = [Buf(f"ws{i}") for i in range(NSLOT)]
        wpool = sb("wpool", [128, 4, 128], BF16)
        wpool_b = Buf("wpool")
        gain_t = [sb(f"gain{i}", [128, D], F32) for i in range(2)]
        gain_b = [Buf(f"gain{i}") for i in range(2)]
        scanmask_t = sb("scanmask", [128, 512], BF16)
        ident = sb("ident", [128, 128], BF16)
        ones = sb("ones", [128, 128], BF16)
        ones512 = sb("ones512", [128, 512], BF16)
        mask4 = sb("mask4", [128, 512], BF16)
        cb_b = Buf("constb")
        small = sb("small", [128, 32], F32)
        small_b = Buf("small")
        lbl = sb("lbl", [128, 8], F32)
        gno = sb("gno", [128, 4], F32)
        psc = sb("psc", [128, 4], F32)
        icnt = sb("icnt", [128, 4, 16], F32)
        Sst = sb("Sst", [128, 2, 4, 128], F32)
        S_b = [[Buf(f"S{pp}{h}") for h in range(4)] for pp in range(2)]
        s_cur = [0, 0, 0, 0]
        smid = sb("smid", [128, 4, 2, 128], BF16)
        smid_b = [[Buf(f"smid{h}{j}") for j in range(2)] for h in range(4)]
        stat = sb("stat", [128, 64], F32)
        stat_b = [Buf(f"stat{i}") for i in range(8)]
        mhalf = sb("mhalf", [128, 8], F32)
        hvec = sb("hvec", [128, 4, 48], F32)
        hvec_b = [Buf(f"hvec{h}") for h in range(4)]
        halo = sb("halo", [128, 4, 16], F32)
        halo_b = Buf("halo")
        kT_t = sb("kT", [128, 8, 256], BF16)
        vmem_t = sb("vmem", [128, 2, D], BF16)
        kT = View(kT_t[:], [Buf("kT")])
        vmem = View(vmem_t[:], [Buf("vmem")])

        SCR = 91 * 1024
        ar = Arena(nc, es, "scr", SCR)

        banks_t = [es.enter_context(nc.psum_tensor(f"bank{i}", [128, 512], F32)) for i in range(8)]

        class Bank:
            def __init__(self, i):
                self.f = banks_t[i][:]
                self.bf = banks_t[i][:].bitcast(BF16)
                self.buf = Buf(f"bank{i}")

        banks = [Bank(i) for i in range(8)]
        bank_ctr = [0]

        def next_bank():
            b = banks[bank_ctr[0] % 8]
            bank_ctr[0] += 1
            return b

        hT = [ar.alloc(8192, BF16, "p (k t) -> p k t", k=8) for _ in range(2)]
        HB0 = ar.cur
        hb = [ar.alloc(2048, BF16) for _ in range(4)]
        HB1 = ar.cur
        ar.cur = HB0
        qe = [ar.alloc(1024, BF16) for _ in range(4)]
        sT = [ar.alloc(1024, BF16) for _ in range(4)]
        ke = [ar.alloc(1024, BF16) for _ in range(2)]
        kd = [ar.alloc(1024, BF16) for _ in range(2)]
        kdT = [ar.alloc(1024, BF16, "p (n k) -> p n k", n=4) for _ in range(2)]
        assert ar.cur >= HB1
        junk = hb[3]
        PH = ar.cur
        vtok = ar.alloc(4096, BF16, "p (n c) -> p n c", n=4)
        gate = [ar.alloc(1024, BF16) for _ in range(4)]
        mixedT = ar.alloc(8192, BF16, "p (k t) -> p k t", k=8)
        pbuf = [ar.alloc(4 * 528, F32) for g in range(4)]
        kq_k = [ar.alloc(2048, F32) for _ in range(4)]
        kq_q = [ar.alloc(2048, F32) for _ in range(4)]
        tC = [ar.alloc(4 * 528, F32) for _ in range(2)]
        tD = [ar.alloc(4 * 528, F32) for _ in range(2)]
        sgt = View(tC[1].ap[:, 0:512], tC[1].bufs)
        pooled = [ar.alloc(1024, BF16) for _ in range(4)]
        P1_END = ar.cur
        ar.cur = PH
        memst = [ar.alloc(4096, F32) for _ in range(2)]
        xqT = ar.alloc(8192, BF16, "p (j t) -> p j t", j=8)
        pT = [ar.alloc(2048, BF16, "p (m t) -> p m t", m=2) for _ in range(4)]
        attT = ar.alloc(8192, BF16, "p (j t) -> p j t", j=8)
        rden = [ar.alloc(2048, F32) for _ in range(2)]
        ar.cur = PH
        uT = [ar.alloc(8192, BF16, "p (j t) -> p j t", j=8) for _ in range(2)]
        rtmp = [ar.alloc(2048, F32) for _ in range(3)]
        cst_stage = ar.view(PH + 40 * 1024, 4 * (128 + 512), F32)

        def wsrc_rowsplit(w, c0):
            return w.rearrange("(k p) n -> p k n", p=128)[:, :, c0:c0 + 512]

        units = {}
        for i, nm in enumerate(["in_q", "in_f", "in_i", "in_g", "in_p"]):
            units[nm] = wsrc_rowsplit(w_in_d, 512 * i)
        for nm, w in (("out", w_out_d), ("xq", w_xq_d), ("xk", w_xk_d), ("xv", w_xv_d), ("xo", w_xo_d)):
            for hf in range(2):
                units[f"{nm}{hf}"] = wsrc_rowsplit(w, 512 * hf)
        for j in range(8):
            units[f"ff1_{j}"] = wsrc_rowsplit(w_ff1_d, 512 * j)
        w2v = w_ff2_d.rearrange("(q f p) n -> q p f n", f=8, p=128)
        for q in range(4):
            for hf in range(2):
                units[f"ff2_{q}{hf}"] = w2v[q][:, :, 512 * hf:512 * hf + 512]

        def st_seq(st):
            s = []
            if st == 1:
                s += ["in_f", "in_i", "in_p"]
            s += ["in_q", "in_g", "out0", "out1", "xq0", "xq1", "xo0", "xo1"]
            for q in range(4):
                s += [f"ff1_{2 * q}", f"ff1_{2 * q + 1}", f"ff2_{q}0", f"ff2_{q}1"]
            return [(u, st) for u in s]

        load_seq = [("in_f", 0), ("in_i", 0), ("in_p", 0), ("xk0", 0), ("xk1", 0), ("xv0", 0), ("xv1", 0)]
        load_seq += st_seq(0) + st_seq(1)
        load_pos = [0]
        free_slots = list(range(NSLOT))
        loaded = {}

        def pump(startup=False):
            while free_slots and load_pos[0] < len(load_seq):
                key = load_seq[load_pos[0]]
                idx = load_pos[0]
                load_pos[0] += 1
                sl = free_slots.pop(0)
                loaded[key] = sl
                extra = []
                if startup and idx >= 2:
                    extra = xbuf[0:4] if idx < 4 else xbuf[0:8]
                S.dma("pool", wslots[:, sl], units[key[0]], f"w{sl}", reads=extra, writes=[wslot_buf[sl]],
                      name=f"ld_{key}")

        def W(name, st):
            key = (name, st)
            assert key in loaded, f"weight unit {key} not loaded yet"
            sl = loaded[key]
            return wslots[:, sl], wslot_buf[sl]

        def release(name, st):
            sl = loaded.pop((name, st))
            free_slots.append(sl)
            pump()

        for i_ in range(4):
            S.dma("sp", xres[:, i_, :], xp_d[128 * i_:128 * i_ + 128, :], f"x{i_}", writes=[xbuf[i_]])
        S.dma("sp", cst_stage.ap, cst_d[:, 0:640], "c0", writes=[cst_stage])
        S.dma("pool", scanmask_t[:], cst_d[:, 640:1152], "c0b", writes=[cb_b])
        vec_b = [Buf(f"vec{i}") for i in range(4)]
        S.dma("sp", lbl[:], lbl_d[:, :], "c1", writes=[vec_b[0]])
        S.dma("sp", gno[:], gno_d[:, :], "c2", writes=[vec_b[1]])
        S.dma("sp", psc[:], psc_d[:, :], "c3", writes=[vec_b[2]])
        S.dma("sp", icnt[:].rearrange("p g t -> p (g t)"), icnt_d[:, :], "c4", writes=[vec_b[3]])
        S.dma("pool", wpool[:], w_pool_d.rearrange("g c d -> c g d"), "c5", writes=[wpool_b])

        def load_gain(slot, row):
            S.dma("sp", gain_t[slot][:], gains_d[row:row + 1, :].partition_broadcast(128), f"g{slot}",
                  writes=[gain_b[slot]])

        G_MIX, G_X, G_MEM, G_FFN, G_FIN = 0, 1, 2, 3, 4
        load_gain(0, G_MIX)
        load_gain(1, G_MEM)

        S.op("dve", lambda e: e.tensor_copy(ident[:], cst_stage.ap[:, 0:128]), reads=[cst_stage], writes=[cb_b])
        S.op("dve", lambda e: e.tensor_copy(mask4[:], cst_stage.ap[:, 128:640]), reads=[cst_stage], writes=[cb_b])
        S.op("pool", lambda e: e.memset(ones[:], 1.0), writes=[cb_b])
        S.op("pool", lambda e: e.memset(ones512[:], 1.0), writes=[cb_b])
        S.op("pool", lambda e: e.memset(mhalf[:], -0.5), writes=[small_b])
        S.op("pool", lambda e: e.memset(Sst[:], 0.0), writes=S_b)
        S.op("dve", lambda e: e.tensor_tensor(small[:, 16:20], lbl[:, 0:4], lbl[:, 4:8], ALU.subtract),
             reads=[vec_b], writes=[small_b])
        S.op("act", lambda e: e.activation(out=small[:, 0:4], in_=small[:, 16:20], func=AF.Sigmoid),
             reads=[small_b], writes=[small_b])
        S.op("dve", lambda e: e.tensor_scalar(small[:, 4:8], small[:, 0:4], -1.0, 1.0, ALU.mult, ALU.add),
             reads=[small_b], writes=[small_b])
        S.op("dve", lambda e: e.tensor_scalar(small[:, 8:12], small[:, 0:4], 1.0, -1.0, ALU.mult, ALU.add),
             reads=[small_b], writes=[small_b])
        oml_c = lambda h: small[:, 4 + h:5 + h]
        noml_c = lambda h: small[:, 8 + h:9 + h]

        stat_ctr = [0]
        evac_ctr = [0]

        def evac_copy(dst_ap, src_ap, reads, writes, eng=None):
            if eng is None:
                eng = "act" if evac_ctr[0] % 2 == 0 else "dve"
                evac_ctr[0] += 1
            if eng == "act":
                S.op("act", lambda e: e.copy(dst_ap, src_ap), reads=reads, writes=writes)
            else:
                S.op("dve", lambda e: e.tensor_copy(dst_ap, src_ap), reads=reads, writes=writes)

        def norm_stats(src_blocks):
            si = stat_ctr[0] % 8
            stat_ctr[0] += 1
            st_ap = stat[:, 8 * si:8 * si + 8]
            stb = stat_b[si]
            nb = len(src_blocks)
            for i, (xap, xb) in enumerate(src_blocks):
                S.op("act", lambda e, xap=xap, i=i: e.activation(out=hb[i].ap, in_=xap, func=AF.Square,
                                                               accum_out=st_ap[:, i:i + 1]),
                     reads=[xb], writes=[hb[i], stb])
            S.op("act", lambda e: e.activation(out=st_ap[:, 0:nb], in_=st_ap[:, 0:nb], func=AF.Ln, scale=1.0 / D, bias=EPS),
                 reads=[stb], writes=[stb])
            S.op("act", lambda e: e.activation(out=st_ap[:, 4:4 + nb], in_=st_ap[:, 0:nb], func=AF.Exp, scale=-0.5),
                 reads=[stb], writes=[stb])
            return (lambda i: st_ap[:, 4 + i:5 + i]), stb

        def norm_elem(src_blocks, gslot):
            rs, stb = norm_stats(src_blocks)
            for i, (xap, xb) in enumerate(src_blocks):
                S.op("dve", lambda e, xap=xap, i=i: e.scalar_tensor_tensor(
                    hb[i].ap, xap, rs(i), gain_t[gslot][:], ALU.mult, ALU.mult),
                    reads=[xb, stb, gain_b[gslot]], writes=[hb[i]])

        def norm_T(nb, hT_dst):
            w = nb * 128
            for kp in range(4):
                bk = next_bank()
                for kk in range(2):
                    kc = 2 * kp + kk
                    for i in range(nb):
                        S.op("pe", lambda e, bk=bk, kk=kk, i=i, kc=kc: e.transpose(
                            bk.bf[:, kk * 512 + i * 128: kk * 512 + (i + 1) * 128],
                            hb[i].ap[:, kc * 128:(kc + 1) * 128], ident[:]),
                            reads=[hb[i], cb_b], writes=[bk.buf])
                for kk in range(2):
                    kc = 2 * kp + kk
                    evac_copy(hT_dst.ap[:, kc, 0:w], bk.bf[:, kk * 512:kk * 512 + w], [], [bk.buf, hT_dst.bufs[kc]],
                              eng=("act" if kk == 0 else "dve"))

        def norm_tile(src_blocks, gslot, hT_dst):
            norm_elem(src_blocks, gslot)
            norm_T(len(src_blocks), hT_dst)

        class Hooks:
            def __init__(self, a=None, b=None):
                self.a = a
                self.b = b

            def run_a(self):
                if self.a is not None:
                    self.a()
                    self.a = None

            def run_b(self):
                if self.b is not None:
                    self.b()
                    self.b = None

        def norm_hooks(src_blocks, gslot, hT_dst):
            return Hooks(lambda: norm_elem(src_blocks, gslot), lambda: norm_T(len(src_blocks), hT_dst))

        NOHOOK = Hooks()

        def proj_B(unit_ap, unit_buf, mchunk, hT_src, ncols=512, col0=0):
            bk = next_bank()
            for kc in range(8):
                S.op("pe", lambda e, bk=bk, kc=kc: e.matmul(
                    bk.f[:, 0:ncols], unit_ap[:, kc, 128 * mchunk:128 * mchunk + 128],
                    hT_src.ap[:, kc, col0:col0 + ncols], start=(kc == 0), stop=(kc == 7)),
                    reads=[unit_buf, hT_src.bufs[kc]], writes=[bk.buf])
            return bk

        def proj_A(lhs_view, blk, unit_ap, unit_buf, order=(0, 1, 2, 3, 4, 5, 6, 7)):
            bk = next_bank()
            for idx, j in enumerate(order):
                S.op("pe", lambda e, bk=bk, j=j, idx=idx: e.matmul(
                    bk.f[:, :], lhs_view.ap[:, j, 128 * blk:128 * blk + 128], unit_ap[:, j, :],
                    start=(idx == 0), stop=(idx == 7)),
                    reads=[unit_buf, lhs_view.bufs[j]], writes=[bk.buf])
            return bk

        def resid_add(bk, blk, hf):
            xa = xres[:, blk, 512 * hf:512 * hf + 512]
            S.op("dve", lambda e: e.tensor_tensor(xa, xa, bk.f[:, :], ALU.add),
                 reads=[xbuf[blk]], writes=[xbuf[blk], bk.buf])

        def load_x_block(src_d, row0, blk):
            S.dma("sp", xres[:, blk, :], src_d[row0:row0 + 128, :], f"x{blk}", writes=[xbuf[blk]])

        class HC:
            pass

        def hg_ln(c):
            S.op("act", lambda e: e.activation(out=c.Cv, in_=kq_k[c.h].ap, func=AF.Ln, scale=-1.0, bias=1.0),
                 reads=[kq_k[c.h]], writes=[c.tC])

        def hg_scan(c):
            S.op("dve", lambda e: e.tensor_tensor_scan(out=c.Dv, data0=scanmask_t[:], data1=c.Cv, initial=0.0,
                                                       op0=ALU.mult, op1=ALU.add),
                 reads=[c.tC, cb_b], writes=[c.tD])

        def hg_extract(c, main):
            hv, d3 = c.hv, c.d3
            if main:
                S.op("dve", lambda e: e.tensor_copy(hv[:, 0:16].rearrange("p (k c) -> p c k", k=2), d3[:, :, 31::32]),
                     reads=[c.tD], writes=[c.hvb])
                S.op("dve", lambda e: e.tensor_tensor(hv[:, 16:24], hv[:, 8:16], hv[:, 0:8], ALU.subtract),
                     reads=[c.hvb], writes=[c.hvb])
            else:
                S.op("dve", lambda e: e.tensor_copy(hv[:, 8:16], d3[:, :, 63]), reads=[c.tD], writes=[c.hvb])
                S.op("dve", lambda e: e.tensor_copy(hv[:, 0:8], d3[:, :, 63]), reads=[c.tD], writes=[c.hvb])

        def hg_exp_small(c, main):
            hv = c.hv
            if main:
                S.op("act", lambda e: e.activation(out=hv[:, 24:48], in_=hv[:, 0:24], func=AF.Exp),
                     reads=[c.hvb], writes=[c.hvb])
            else:
                S.op("act", lambda e: e.activation(out=hv[:, 8:16], in_=hv[:, 8:16], func=AF.Exp),
                     reads=[c.hvb], writes=[c.hvb])

        def hg_sub(c, main):
            col = 0
            S.op("dve", lambda e: e.tensor_tensor(
                c.d3, c.d3, c.hv[:, col:col + 8].rearrange("p (c o) -> p c o", o=1).to_broadcast([128, 8, 64]),
                ALU.subtract), reads=[c.tD, c.hvb], writes=[c.tD])

        def hg_exp_big(c, main):
            if main:
                S.op("act", lambda e: e.activation(out=c.Cv, in_=c.Dv, func=AF.Exp), reads=[c.tD], writes=[c.tC])
            S.op("act", lambda e: e.activation(out=c.Dv, in_=c.Dv, func=AF.Exp, scale=-1.0), reads=[c.tD], writes=[c.tD])

        def hg_mults(c, main):
            h, r = c.h, c.r
            if main:
                S.op("dve", lambda e: e.tensor_tensor(qe[h].ap, kq_q[h].ap, c.Cv, ALU.mult),
                     reads=[kq_q[h], c.tC], writes=[qe[h]])
                S.op("dve", lambda e: e.tensor_tensor(ke[r].ap, kq_k[h].ap, c.Dv, ALU.mult),
                     reads=[kq_k[h], c.tD], writes=[ke[r]])
                ke3 = ke[r].ap.rearrange("p (c j) -> p c j", j=64)
                kd3 = kd[r].ap.rearrange("p (c j) -> p c j", j=64)
                S.op("dve", lambda e: e.tensor_tensor(
                    kd3, ke3, c.hv[:, 40:48].rearrange("p (c o) -> p c o", o=1).to_broadcast([128, 8, 64]), ALU.mult),
                    reads=[ke[r], c.hvb], writes=[kd[r]])
            else:
                S.op("dve", lambda e: e.tensor_tensor(kd[r].ap, kq_k[h].ap, c.Dv, ALU.mult),
                     reads=[kq_k[h], c.tD], writes=[kd[r]])

        def hg_pe_front(c, main):
            h, r = c.h, c.r
            if main:
                sb_ = next_bank()
                for n in range(4):
                    S.op("pe", lambda e, n=n: e.matmul(
                        sb_.f[:, 128 * n:128 * n + 128], ke[r].ap[:, 128 * n:128 * n + 128],
                        qe[h].ap[:, 128 * n:128 * n + 128], start=True, stop=True),
                        reads=[ke[r], qe[h]], writes=[sb_.buf])
                S.op("dve", lambda e: e.tensor_tensor(sT[h].ap, sb_.f[:, :], mask4[:], ALU.mult),
                     reads=[cb_b], writes=[sb_.buf, sT[h]])
            bk = next_bank()
            for n in range(4):
                S.op("pe", lambda e, n=n: e.transpose(bk.bf[:, 128 * n:128 * n + 128],
                                                      kd[r].ap[:, 128 * n:128 * n + 128], ident[:]),
                     reads=[kd[r], cb_b], writes=[bk.buf])
            evac_copy(kdT[r].ap.rearrange("p n k -> p (n k)"), bk.bf[:, 0:512], [], [bk.buf, kdT[r]], eng="act")

        def hg_pe_state(c):
            h, r = c.h, c.r
            c.pbanks = [next_bank(), next_bank()]
            for ch in range(8):
                n, half = divmod(ch, 2)
                pb = c.pbanks[half]
                S.op("pe", lambda e, pb=pb, n=n, half=half: e.matmul(
                    pb.f[:, n * 128:n * 128 + 128],
                    kdT[r].ap[64 * half:64 * half + 64, n, :],
                    vtok.ap[64 * half:64 * half + 64, n, 128 * h:128 * h + 128], start=True, stop=True),
                    reads=[kdT[r], vtok.bufs[n]], writes=[pb.buf])

        def hg_state_step(c, ch):
            h = c.h
            cur = s_cur[h]
            nxt = 1 - cur
            pb = c.pbanks[ch % 2]
            S.op("dve", lambda e: e.scalar_tensor_tensor(
                Sst[:, nxt, h, :], Sst[:, cur, h, :], c.hv[:, 32 + ch:33 + ch], pb.f[:, (ch // 2) * 128:(ch // 2) * 128 + 128],
                ALU.mult, ALU.add),
                reads=[S_b[cur][h], c.hvb], writes=[S_b[nxt][h], pb.buf])
            s_cur[h] = nxt

        def hg_smid(c, ch):
            h = c.h
            cur = s_cur[h]
            j = ch % 2
            S.op("pool", lambda e: e.tensor_scalar(smid[:, h, j, :], Sst[:, cur, h, :], c.hv[:, 24 + ch:25 + ch], 0.0,
                                                   ALU.mult, ALU.add),
                 reads=[S_b[cur][h], c.hvb], writes=[smid_b[h][j]])

        def pf_front_stages(cs):
            def st_ln():
                for c in cs:
                    hg_ln(c)

            def st_scan():
                for c in cs:
                    S.op("dve", lambda e, c=c: e.tensor_tensor_scan(out=c.Dv, data0=ones512[:], data1=c.Cv, initial=0.0,
                                                                op0=ALU.mult, op1=ALU.add),
                         reads=[c.tC, cb_b], writes=[c.tD])

            def st_extract():
                for c in cs:
                    S.op("dve", lambda e, c=c: e.tensor_copy(c.hv[:, 0:1], c.Dv[:, 511:512]), reads=[c.tD], writes=[c.hvb])

            def st_exp_small():
                for c in cs:
                    S.op("act", lambda e, c=c: e.activation(out=c.hv[:, 8:9], in_=c.hv[:, 0:1], func=AF.Exp),
                         reads=[c.hvb], writes=[c.hvb])

            def st_sub():
                for c in cs:
                    S.op("dve", lambda e, c=c: e.tensor_scalar(c.Dv, c.Dv, c.hv[:, 0:1], None, ALU.subtract),
                         reads=[c.tD, c.hvb], writes=[c.tD])

            def st_exp_big():
                for c in cs:
                    hg_exp_big(c, False)

            def st_mults():
                for c in cs:
                    hg_mults(c, False)
            return [st_ln, st_scan, st_extract, st_exp_small, st_sub, st_exp_big, st_mults]

        def pf_pe(cs):
            for c in cs:
                hg_pe_front(c, False)
            for c in cs:
                h, r = c.h, c.r
                c.pb = next_bank()
                for n in range(4):
                    S.op("pe", lambda e, c=c, n=n, h=h, r=r: e.matmul(
                        c.pb.f[:, 0:128], kdT[r].ap[:, n, :], vtok.ap[:, n, 128 * h:128 * h + 128],
                        start=(n == 0), stop=(n == 3)),
                        reads=[kdT[r], vtok.bufs[n]], writes=[c.pb.buf])

        def pf_step(cs):
            for c in cs:
                h = c.h
                cur = s_cur[h]
                nxt = 1 - cur
                S.op("dve", lambda e, c=c, h=h, cur=cur, nxt=nxt: e.scalar_tensor_tensor(
                    Sst[:, nxt, h, :], Sst[:, cur, h, :], c.hv[:, 8:9], c.pb.f[:, 0:128], ALU.mult, ALU.add),
                    reads=[S_b[cur][h], c.hvb], writes=[S_b[nxt][h], c.pb.buf])
                s_cur[h] = nxt

        def make_hc(h):
            c = HC()
            c.h = h
            c.r = h % 2
            c.hv = hvec[:, h, :]
            c.hvb = hvec_b[h]
            c.tC = tC[c.r]
            c.tD = tD[c.r]
            c.Cv = tC[c.r].ap[:, 0:512]
            c.Dv = tD[c.r].ap[:, 0:512]
            c.d3 = c.Dv.rearrange("p (c j) -> p c j", j=64)
            return c

        out_sems = set()
        try:
            def prefix_norm(q):
                blks = [4 * (q % 2) + i for i in range(4)]
                norm_tile([(xres[:, b, :], xbuf[b]) for b in blks], 0, hT[q % 2])

            for i in range(4):
                load_x_block(xp_d, 512 + 128 * i, 4 + i)
            pump(startup=True)
            prefix_norm(0)
            for q in range(4):
                hTq = hT[q % 2]
                if q + 1 < 4:
                    nblks = [4 * ((q + 1) % 2) + i for i in range(4)]
                    norm_elem([(xres[:, b, :], xbuf[b]) for b in nblks], 0)
                for i in range(4):
                    blk = 4 * (q % 2) + i
                    if q + 2 < 4:
                        load_x_block(xp_d, 512 * (q + 2) + 128 * i, blk)
                    else:
                        load_x_block(x_d, 128 * blk, blk)
                uf, ufb = W("in_f", 0)
                ui, uib = W("in_i", 0)
                for h in range(4):
                    bk = proj_B(uf, ufb, h, hTq)
                    S.op("act", lambda e, bk=bk, h=h: e.activation(out=kq_k[h].ap, in_=bk.f[:, :], func=AF.Sigmoid),
                         writes=[bk.buf, kq_k[h]])
                    S.op("dve", lambda e, h=h: e.tensor_scalar(kq_k[h].ap, kq_k[h].ap, noml_c(h), oml_c(h), ALU.mult, ALU.add),
                         reads=[kq_k[h], small_b], writes=[kq_k[h]])
                cs0 = [make_hc(0), make_hc(1)]
                cs1 = [make_hc(2), make_hc(3)]
                PF0 = pf_front_stages(cs0)
                PF0[0]()
                PF0[1]()
                for n in range(4):
                    bk = proj_A(hTq, n, ui, uib)
                    evac_copy(vtok.ap[:, n, :], bk.f[:, :], [], [bk.buf, vtok.bufs[n]])
                if q + 1 < 4:
                    norm_T(4, hT[(q + 1) % 2])
                if q == 3:
                    up, upb = W("in_p", 0)
                    for g in range(4):
                        bk = proj_B(up, upb, g, hTq, ncols=16, col0=496)
                        S.op("act", lambda e, bk=bk, g=g: e.copy(halo[:, g, :], bk.f[:, 0:16]),
                             writes=[bk.buf, halo_b])
                for th in PF0[2:]:
                    th()
                pf_pe(cs0)
                for th in pf_front_stages(cs1):
                    th()
                pf_step(cs0)
                pf_pe(cs1)
                pf_step(cs1)
                ckpt(f"prefix{q}")

            for mb in range(2):
                S.dma("sp", memst[mb].ap, mem_d[128 * mb:128 * mb + 128, :], f"m{mb}", writes=[memst[mb]])
            norm_tile([(memst[mb].ap, memst[mb]) for mb in range(2)], 1, hT[0])
            hmT = hT[0]
            for uh in range(2):
                uk, ukb = W(f"xk{uh}", 0)
                for jp in range(2):
                    bk = next_bank()
                    for jj in range(2):
                        j4 = 2 * jp + jj
                        for kc in range(8):
                            S.op("pe", lambda e, bk=bk, kc=kc, jj=jj, j4=j4, uk=uk: e.matmul(
                                bk.f[:, 256 * jj:256 * jj + 256], uk[:, kc, 128 * j4:128 * j4 + 128], hmT.ap[:, kc, 0:256],
                                start=(kc == 0), stop=(kc == 7)),
                                reads=[ukb, hmT], writes=[bk.buf])
                    j0 = 4 * uh + 2 * jp
                    evac_copy(kT.ap[:, j0:j0 + 2, :].rearrange("p j m -> p (j m)"), bk.f[:, :], [], [bk.buf, kT])
            for mb in range(2):
                for hf in range(2):
                    uv, uvb = W(f"xv{hf}", 0)
                    bk = proj_A(hmT, mb, uv, uvb)
                    evac_copy(vmem.ap[:, mb, 512 * hf:512 * hf + 512], bk.f[:, :], [], [bk.buf, vmem])
            for nm in ("xk0", "xk1", "xv0", "xv1"):
                release(nm, 0)
            load_gain(1, G_X)
            ckpt("kv")


            def mixer_tile(st, t, hk=NOHOOK, late=False):
                hTt = hT[t]
                blks = [4 * t + i for i in range(4)]
                uq, uqb = W("in_q", st)
                uf, ufb = W("in_f", st)
                ui, uib = W("in_i", st)
                ug, ugb = W("in_g", st)
                up, upb = W("in_p", st)
                for g in range(4):
                    bk = proj_B(up, upb, g, hTt)
                    S.op("act", lambda e, bk=bk, g=g: e.copy(pbuf[g].ap[:, 16:528], bk.f[:, :]),
                         writes=[bk.buf, pbuf[g]])
                    S.op("pool", lambda e, g=g: e.tensor_copy(pbuf[g].ap[:, 0:16], halo[:, g, :]),
                         reads=[halo_b], writes=[pbuf[g]])
                for g in range(4):
                    wlen = POOL_W[g]
                    cur = pbuf[g].ap
                    curv = pbuf[g]
                    sh = 1
                    tmps = [tC[0], tD[0]]
                    ti = 0
                    while sh < wlen:
                        dst = tmps[ti % 2]
                        lo = 2 * sh - 1
                        S.op("pool", lambda e, dst=dst, cur=cur, sh=sh, lo=lo: e.tensor_tensor(
                            dst.ap[:, lo:528], cur[:, lo:528], cur[:, lo - sh:528 - sh], ALU.add),
                            reads=[curv], writes=[dst])
                        cur = dst.ap
                        curv = dst
                        ti += 1
                        sh *= 2
                    if t == 0 and st == 0:
                        S.op("pool", lambda e, g=g, cur=cur: e.tensor_tensor(
                            cur[:, 16:32], cur[:, 16:32], icnt[:, g, :], ALU.mult),
                            reads=[curv, vec_b], writes=[curv])
                        S.op("pool", lambda e, g=g, cur=cur, wlen=wlen: e.tensor_scalar(
                            cur[:, 32:528], cur[:, 32:528], 1.0 / wlen, 0.0, ALU.mult, ALU.add),
                            reads=[curv], writes=[curv])
                    else:
                        S.op("pool", lambda e, g=g, cur=cur, wlen=wlen: e.tensor_scalar(
                            cur[:, 16:528], cur[:, 16:528], 1.0 / wlen, 0.0, ALU.mult, ALU.add),
                            reads=[curv], writes=[curv])
                    S.op("pool", lambda e, g=g, cur=cur: e.tensor_tensor(
                        pooled[g].ap, cur[:, 16:528], pbuf[g].ap[:, 16:528], ALU.subtract),
                        reads=[curv, pbuf[g]], writes=[pooled[g]])
                    S.op("pool", lambda e, g=g: e.tensor_copy(halo[:, g, :], pbuf[g].ap[:, 512:528]),
                         reads=[pbuf[g]], writes=[halo_b])
                if not late:
                    hk.run_a()
                for h in range(4):
                    bk = proj_B(uf, ufb, h, hTt)
                    S.op("act", lambda e, bk=bk, h=h: e.activation(out=kq_k[h].ap, in_=bk.f[:, :], func=AF.Sigmoid),
                         writes=[bk.buf, kq_k[h]])
                    S.op("dve", lambda e, h=h: e.tensor_scalar(kq_k[h].ap, kq_k[h].ap, noml_c(h), oml_c(h), ALU.mult, ALU.add),
                         reads=[kq_k[h], small_b], writes=[kq_k[h]])
                for h in range(4):
                    bk = proj_B(uq, uqb, h, hTt)
                    S.op("act", lambda e, bk=bk: e.activation(out=sgt.ap, in_=bk.f[:, :], func=AF.Sigmoid),
                         writes=[bk.buf, sgt])
                    S.op("dve", lambda e, bk=bk, h=h: e.tensor_tensor(kq_q[h].ap, sgt.ap, bk.f[:, :], ALU.mult),
                         reads=[sgt], writes=[bk.buf, kq_q[h]])
                if late:
                    hk.run_a()
                else:
                    hk.run_b()
                def do_poolmm():
                    for g in range(4):
                        bk = next_bank()
                        S.op("pe", lambda e, bk=bk, g=g: e.matmul(bk.f[:, :], wpool[:, g, :], pooled[g].ap, start=True, stop=True),
                             reads=[wpool_b, pooled[g]], writes=[bk.buf])
                        S.op("act", lambda e, bk=bk, g=g: e.activation(out=mixedT.ap[:, 4 + g, :], in_=bk.f[:, :], func=AF.Copy,
                                                                     scale=psc[:, g:g + 1]),
                             reads=[vec_b], writes=[bk.buf, mixedT.bufs[4 + g]])
                def do_g():
                    for h in range(4):
                        bk = proj_B(ug, ugb, h, hTt)
                        S.op("act", lambda e, bk=bk: e.activation(out=sgt.ap, in_=bk.f[:, :], func=AF.Sigmoid),
                             writes=[bk.buf, sgt])
                        S.op("dve", lambda e, bk=bk, h=h: e.tensor_tensor(gate[h].ap, sgt.ap, bk.f[:, :], ALU.mult),
                             reads=[sgt], writes=[bk.buf, gate[h]])
                def do_v():
                    for n in range(4):
                        bk = proj_A(hTt, n, ui, uib)
                        evac_copy(vtok.ap[:, n, :], bk.f[:, :], [], [bk.buf, vtok.bufs[n]])
                def front_stages(cs):
                    L = []
                    for stage in (hg_ln, hg_scan):
                        L.append(lambda stage=stage: [stage(c) for c in cs])
                    for stage in (hg_extract, hg_exp_small, hg_sub, hg_exp_big, hg_mults):
                        L.append(lambda stage=stage: [stage(c, True) for c in cs])
                    return L

                def pe_part(cs):
                    for c in cs:
                        hg_pe_front(c, True)
                    for c in cs:
                        hg_pe_state(c)

                def chain_blocks(cs):
                    for c in cs:
                        c.ob = next_bank()

                    def blk(n):
                        for c in cs:
                            h, r = c.h, c.r
                            S.op("pe", lambda e, c=c, n=n, h=h, r=r: e.matmul(
                                c.ob.f[:, 128 * n:128 * n + 128], vtok.ap[:, n, 128 * h:128 * h + 128],
                                sT[h].ap[:, 128 * n:128 * n + 128], start=True, stop=False),
                                reads=[vtok.bufs[n], sT[h]], writes=[c.ob.buf])
                        for j in range(2):
                            ch = 2 * n + j
                            for c in cs:
                                h, r = c.h, c.r
                                hg_smid(c, ch)
                                S.op("pe", lambda e, c=c, n=n, j=j, h=h, r=r: e.matmul(
                                    c.ob.f[:, 128 * n + 64 * j:128 * n + 64 * j + 64], smid[:, h, j, :],
                                    qe[h].ap[:, 128 * n + 64 * j:128 * n + 64 * j + 64], start=False, stop=(j == 1)),
                                    reads=[smid_b[h][j], qe[h]], writes=[c.ob.buf])
                                hg_state_step(c, ch)
                    return [lambda n=n: blk(n) for n in range(4)]

                def onorm_steps(cs):
                    def s1():
                        for c in cs:
                            S.op("act", lambda e, c=c: e.activation(out=sT[c.h].ap, in_=c.ob.f[:, :], func=AF.Square),
                                 writes=[c.ob.buf, sT[c.h]])
                        for c in cs:
                            c.nb = next_bank()
                            S.op("pe", lambda e, c=c: e.matmul(c.nb.f[:, :], ones[:], sT[c.h].ap, start=True, stop=True),
                                 reads=[cb_b, sT[c.h]], writes=[c.nb.buf])

                    def s2():
                        for c in cs:
                            S.op("act", lambda e, c=c: e.activation(out=kq_k[c.h].ap, in_=c.nb.f[:, :], func=AF.Ln, scale=1.0 / 128, bias=EPS),
                                 writes=[c.nb.buf, kq_k[c.h]])
                        for c in cs:
                            S.op("act", lambda e, c=c: e.activation(out=kq_k[c.h].ap, in_=kq_k[c.h].ap, func=AF.Exp, scale=-0.5),
                                 reads=[kq_k[c.h]], writes=[kq_k[c.h]])

                    def s3():
                        for c in cs:
                            S.op("dve", lambda e, c=c: e.tensor_tensor(kq_q[c.h].ap, kq_k[c.h].ap, c.ob.f[:, :], ALU.mult),
                                 reads=[kq_k[c.h]], writes=[c.ob.buf, kq_q[c.h]])
                        for c in cs:
                            S.op("dve", lambda e, c=c: e.scalar_tensor_tensor(
                                mixedT.ap[:, c.h, :], kq_q[c.h].ap, gno[:, c.h:c.h + 1], gate[c.h].ap, ALU.mult, ALU.mult),
                                reads=[kq_q[c.h], vec_b, gate[c.h]], writes=[mixedT.bufs[c.h]])
                    return [s1, s2, s3]

                def weave(A, B):
                    A = list(A)
                    B = list(B)
                    while A or B:
                        if A:
                            A.pop(0)()
                        if B:
                            B.pop(0)()

                cs0 = [make_hc(0), make_hc(1)]
                cs1 = [make_hc(2), make_hc(3)]
                F0 = front_stages(cs0)
                do_g()
                F0[0]()
                F0[1]()
                do_v()
                hk.run_b()
                for th_ in F0[2:]:
                    th_()
                do_poolmm()
                pe_part(cs0)
                weave(front_stages(cs1), chain_blocks(cs0) + onorm_steps(cs0))
                pe_part(cs1)
                for th in chain_blocks(cs1) + onorm_steps(cs1):
                    th()

            def mixer_out(st, t):
                for n in range(4):
                    for hf in range(2):
                        uo, uob = W(f"out{hf}", st)
                        bk = proj_A(mixedT, n, uo, uob, order=(4, 5, 6, 7, 0, 1, 2, 3))
                        resid_add(bk, 4 * t + n, hf)

            def xattn_tile(st, t, hk=NOHOOK):
                hTt = hT[t]
                hk.run_a()
                for j in range(8):
                    uq, uqb = W(f"xq{j // 4}", st)
                    bk = proj_B(uq, uqb, j % 4, hTt)
                    evac_copy(xqT.ap[:, j, :], bk.f[:, :], [], [bk.buf, xqT.bufs[j]])
                hk.run_b()
                def scores_exp(h):
                    for mc in range(2):
                        bk = next_bank()
                        for ec in range(2):
                            j = 2 * h + ec
                            S.op("pe", lambda e, bk=bk, mc=mc, j=j, ec=ec: e.matmul(
                                bk.f[:, :], kT.ap[:, j, 128 * mc:128 * mc + 128], xqT.ap[:, j, :],
                                start=(ec == 0), stop=(ec == 1)),
                                reads=[kT, xqT.bufs[j]], writes=[bk.buf])
                        S.op("act", lambda e, bk=bk, h=h, mc=mc: e.activation(out=pT[h].ap[:, mc, :], in_=bk.f[:, :],
                                                                           func=AF.Exp, scale=1.0 / 16),
                             writes=[bk.buf, pT[h].bufs[mc]])

                def den_att(h):
                    db = next_bank()
                    for mc in range(2):
                        S.op("pe", lambda e, db=db, h=h, mc=mc: e.matmul(db.f[:, :], ones[:], pT[h].ap[:, mc, :],
                                                                      start=(mc == 0), stop=(mc == 1)),
                             reads=[cb_b, pT[h].bufs[mc]], writes=[db.buf])
                    rd = rden[h % 2]
                    S.op("act", lambda e, db=db, rd=rd: e.activation(out=rd.ap, in_=db.f[:, :], func=AF.Ln),
                         writes=[db.buf, rd])
                    S.op("act", lambda e, rd=rd: e.activation(out=rd.ap, in_=rd.ap, func=AF.Exp, scale=-1.0),
                         reads=[rd], writes=[rd])
                    for ec in range(2):
                        j = 2 * h + ec
                        bk = next_bank()
                        for mc in range(2):
                            S.op("pe", lambda e, bk=bk, mc=mc, j=j, h=h: e.matmul(
                                bk.f[:, :], vmem.ap[:, mc, 128 * j:128 * j + 128], pT[h].ap[:, mc, :],
                                start=(mc == 0), stop=(mc == 1)),
                                reads=[vmem, pT[h].bufs[mc]], writes=[bk.buf])
                        S.op("dve", lambda e, bk=bk, j=j, rd=rd: e.tensor_tensor(attT.ap[:, j, :], rd.ap, bk.f[:, :], ALU.mult),
                             reads=[rd], writes=[bk.buf, attT.bufs[j]])

                scores_exp(0)
                scores_exp(1)
                for h in range(4):
                    if h + 2 < 4:
                        scores_exp(h + 2)
                    den_att(h)
                for n in range(4):
                    for hf in range(2):
                        uo, uob = W(f"xo{hf}", st)
                        bk = proj_A(attT, n, uo, uob)
                        resid_add(bk, 4 * t + n, hf)

            ffn_ctr = [0]

            def ffn_ff1(st, q, t, hk=NOHOOK):
                hTt = hT[t]
                hk.run_a()
                u = uT[ffn_ctr[0] % 2]
                ffn_ctr[0] += 1
                for fc in range(8):
                    u1, u1b = W(f"ff1_{2 * q + fc // 4}", st)
                    bk = proj_B(u1, u1b, fc % 4, hTt)
                    rt = rtmp[fc % 3]
                    S.op("act", lambda e, bk=bk, rt=rt: e.activation(out=rt.ap, in_=bk.f[:, :], func=AF.Relu),
                         writes=[bk.buf, rt])
                    S.op("pool", lambda e, rt=rt, fc=fc, u=u: e.tensor_tensor(u.ap[:, fc, :], rt.ap, rt.ap, ALU.mult),
                         reads=[rt], writes=[u.bufs[fc]])
                    if fc == 6:
                        hk.run_b()
                return u

            def ffn_ff2(st, q, t, u, hk=NOHOOK):
                hk.run_a()
                for n in range(4):
                    for hf in range(2):
                        u2, u2b = W(f"ff2_{q}{hf}", st)
                        bk = proj_A(u, n, u2, u2b)
                        resid_add(bk, 4 * t + n, hf)
                    if n == 1:
                        hk.run_b()

            def final_tile(st, t, gslot):
                blks = [4 * t + i for i in range(4)]
                rs, stb = norm_stats([(xres[:, b, :], xbuf[b]) for b in blks])
                for i, b in enumerate(blks):
                    S.op("dve", lambda e, b=b, i=i: e.scalar_tensor_tensor(
                        xres[:, b, :], xres[:, b, :], rs(i), gain_t[gslot][:], ALU.mult, ALU.mult),
                        reads=[xbuf[b], stb, gain_b[gslot]], writes=[xbuf[b]])
                    row0 = 1024 * st + 128 * b
                    S.dma("sp", y_d[row0:row0 + 128, :], xres[:, b, :], f"x{b}", reads=[xbuf[b]])
                    out_sems.add(f"x{b}")

            def x_blocks(t):
                return [(xres[:, 4 * t + i, :], xbuf[4 * t + i]) for i in range(4)]

            for st in range(2):
                if st == 0:
                    norm_tile(x_blocks(0), 0, hT[0])
                mixer_tile(st, 0, norm_hooks(x_blocks(1), 0, hT[1]), late=(st == 1))
                mixer_out(st, 0)
                mixer_tile(st, 1, norm_hooks(x_blocks(0), 1, hT[0]))
                for nm in ("in_q", "in_f", "in_i", "in_g", "in_p"):
                    release(nm, st)
                mixer_out(st, 1)
                release("out0", st)
                release("out1", st)
                load_gain(0, G_FFN)
                ckpt(f"p1_{st}")
                xattn_tile(st, 0, norm_hooks(x_blocks(1), 1, hT[1]))
                xattn_tile(st, 1, norm_hooks(x_blocks(0), 0, hT[0]))
                for nm in ("xq0", "xq1", "xo0", "xo1"):
                    release(nm, st)
                load_gain(1, G_FIN)
                ckpt(f"p2_{st}")
                items = [(q, t) for q in range(4) for t in range(2)]
                us = {items[0]: ffn_ff1(st, 0, 0, norm_hooks(x_blocks(1), 0, hT[1]))}
                for i, (q, t) in enumerate(items):
                    if i + 1 < len(items):
                        q2, t2 = items[i + 1]
                        us[(q2, t2)] = ffn_ff1(st, q2, t2)
                    hk = NOHOOK
                    if (q, t) == (3, 1) and st == 0:
                        load_gain(0, G_MIX)
                        hk = norm_hooks(x_blocks(0), 0, hT[0])
                    ffn_ff2(st, q, t, us[(q, t)], hk)
                    if q == 3:
                        final_tile(st, t, 1)
                        if st == 0:
                            for i_ in range(4):
                                load_x_block(x_d, 1024 + 128 * (4 * t + i_), 4 * t + i_)
                    if t == 1:
                        for nm in (f"ff1_{2 * q}", f"ff1_{2 * q + 1}", f"ff2_{q}0", f"ff2_{q}1"):
                            release(nm, st)
                        ckpt(f"f{q}_{st}")
                if st == 0:
                    load_gain(1, G_X)


        except _Stop:
            for b_ in range(8):
                S.dma("sp", y_d[128 * b_:128 * b_ + 128, :], xres[:, b_, :], f"x{b_}", reads=[xbuf[b_]])
                out_sems.add(f"x{b_}")
        stats = S.emit(final_wait_sems=sorted(out_sems))
        build_program.stats = stats
    return nc


_CACHE = {}


def _constants():
    ident = np.eye(128, dtype=np.float32)
    s = np.arange(128)[:, None]
    t = np.arange(128)[None, :]
    m = ((t >= s) & ((s // 64) == (t // 64))).astype(np.float32)
    mask4 = np.tile(m, (1, 4))
    scan = np.ones((128, 512), np.float32)
    scan[:, ::64] = 0.0
    return np.ascontiguousarray(np.concatenate([ident, mask4, scan], axis=1))


def kernel(x, mem, norm_mix_g, w_in, lb_logits, hgrn_norm_g, w_pool, pool_scale, w_out,
           norm_x_g, norm_mem_g, w_xq, w_xk, w_xv, w_xo, norm_ffn_g, w_ff1, w_ff2, final_norm_g):
    f32 = np.float32
    x = np.asarray(x, f32)
    mem = np.asarray(mem, f32)
    if "nc" not in _CACHE:
        _CACHE["nc"] = build_program()
    nc = _CACHE["nc"]
    c = lambda a: np.ascontiguousarray(np.asarray(a, f32))
    gains = c(np.stack([np.asarray(norm_mix_g)[0], np.asarray(norm_x_g)[0], np.asarray(norm_mem_g)[0],
                        np.asarray(norm_ffn_g)[0], np.asarray(final_norm_g)]))
    lbl = c(np.asarray(lb_logits, f32).reshape(2, 4, 128).transpose(2, 0, 1).reshape(128, 8))
    gno = c(np.asarray(hgrn_norm_g, f32)[0].T)
    psc = c(np.asarray(pool_scale, f32)[0].reshape(4, 128).T)
    shared = {
        "w_in": c(np.asarray(w_in)[0]), "w_out": c(np.asarray(w_out)[0]),
        "w_xq": c(np.asarray(w_xq)[0].reshape(D, D)), "w_xk": c(np.asarray(w_xk)[0].reshape(D, D)),
        "w_xv": c(np.asarray(w_xv)[0].reshape(D, D)), "w_xo": c(np.asarray(w_xo)[0].reshape(D, D)),
        "w_ff1": c(np.asarray(w_ff1)[0]), "w_ff2": c(np.asarray(w_ff2)[0]),
        "w_pool": c(np.asarray(w_pool)[0]), "gains": gains, "lbl": lbl, "gno": gno, "psc": psc,
        "cst": _constants(),
    }
    icnt_first = np.zeros((4, 16), f32)
    icnt_rest = np.zeros((4, 16), f32)
    for g, w in enumerate(POOL_W):
        for t in range(16):
            icnt_first[g, t] = 1.0 / min(t + 1, w)
            icnt_rest[g, t] = 1.0 / w
    in_maps = []
    for core in range(8):
        b, half = divmod(core, 2)
        m = dict(shared)
        m["x"] = c(x[b, half * NTOK:(half + 1) * NTOK])
        m["xp"] = c(x[b, 0:NTOK]) if half == 1 else np.zeros((NTOK, D), f32)
        m["mem"] = c(mem[b])
        ic = icnt_first if half == 0 else icnt_rest
        m["icnt"] = c(np.broadcast_to(ic.reshape(1, 64), (128, 64)))
        in_maps.append(m)
    res = run_bass_kernel_spmd(nc, in_maps, core_ids=list(range(8)))
    out = np.empty((4, 4096, D), f32)
    for core in range(8):
        b, half = divmod(core, 2)
        out[b, half * NTOK:(half + 1) * NTOK] = res.results[core]["y"]
    return out
```

```python
import contextlib
import math

import numpy as np

import concourse.bass as bass
import concourse.mybir as mybir
from concourse.bass_utils import run_bass_kernel_spmd

F32 = mybir.dt.float32
BF16 = mybir.dt.bfloat16
AF = mybir.ActivationFunctionType
ALU = mybir.AluOpType

ENGS = ("pe", "act", "dve", "pool", "sp")

D = 1024
NTOK = 2048
TILE = 512
EPS = 1e-6
POOL_W = (2, 4, 8, 16)


class Buf:
    __slots__ = ("name", "last_w", "readers")

    def __init__(self, name):
        self.name = name
        self.last_w = None
        self.readers = {}


class View:
    __slots__ = ("ap", "bufs")

    def __init__(self, ap, bufs):
        self.ap = ap
        self.bufs = list(bufs)


class Op:
    __slots__ = ("eng", "fn", "deps", "signal", "sig_count", "dma_sem", "name")

    def __init__(self, eng, fn, name=""):
        self.eng = eng
        self.fn = fn
        self.deps = []
        self.signal = False
        self.sig_count = None
        self.dma_sem = None
        self.name = name


def _flat(items):
    out = []
    for it in items:
        if it is None:
            continue
        if isinstance(it, Buf):
            out.append(it)
        elif isinstance(it, View):
            out.extend(it.bufs)
        else:
            out.extend(_flat(it))
    return out


class Sched:
    def __init__(self, nc):
        self.nc = nc
        self.eng_ops = {e: [] for e in ENGS}
        self.dma_counts = {}

    def _add_dep(self, op, prod):
        if prod is None or prod is op:
            return
        if prod.dma_sem is None:
            if prod.eng == op.eng and op.eng in ("pe", "sp"):
                return
            prod.signal = True
        op.deps.append(prod)

    def op(self, eng, fn, reads=(), writes=(), name=""):
        reads = _flat(reads)
        writes = _flat(writes)
        o = Op(eng, fn, name)
        for b in reads:
            self._add_dep(o, b.last_w)
        for b in writes:
            self._add_dep(o, b.last_w)
            for r in b.readers.values():
                self._add_dep(o, r)
        for b in reads:
            b.readers[eng] = o
        for b in writes:
            b.last_w = o
            b.readers = {}
        self.eng_ops[eng].append(o)
        return o

    def dma(self, queue, out, in_, semkey, reads=(), writes=(), name=""):
        reads = _flat(reads)
        writes = _flat(writes)

        def fn(e):
            return e.dma_start(out=out, in_=in_)

        o = Op(queue, fn, name)
        o.dma_sem = semkey
        n = self.dma_counts.get(semkey, 0) + 1
        self.dma_counts[semkey] = n
        o.sig_count = 16 * n
        o.signal = True
        for b in reads:
            self._add_dep(o, b.last_w)
        for b in writes:
            self._add_dep(o, b.last_w)
            for r in b.readers.values():
                self._add_dep(o, r)
        for b in reads:
            b.readers[("dma", semkey, n)] = o
        for b in writes:
            b.last_w = o
            b.readers = {}
        self.eng_ops[queue].append(o)
        return o

    def emit(self, final_wait_sems=()):
        nc = self.nc
        for e in ENGS:
            c = 0
            for o in self.eng_ops[e]:
                if o.dma_sem is None and o.signal:
                    c += 1
                    o.sig_count = c
        sems = {}
        for e in ENGS:
            sems[("eng", e)] = nc.alloc_semaphore(name=f"sem_{e}")
        for k in self.dma_counts:
            sems[("dma", k)] = nc.alloc_semaphore(name=f"dsem_{k}")
        stats = {}
        with nc.Block() as block:
            def run(eng_name, handle):
                waited = {}
                nwait = 0
                for o in self.eng_ops[eng_name]:
                    need = {}
                    for p in o.deps:
                        ch = ("dma", p.dma_sem) if p.dma_sem is not None else ("eng", p.eng)
                        v = p.sig_count
                        if v > need.get(ch, 0):
                            need[ch] = v
                    for ch, v in need.items():
                        if waited.get(ch, 0) >= v:
                            continue
                        handle.wait_ge(sems[ch], v)
                        waited[ch] = v
                        nwait += 1
                    ins = o.fn(handle)
                    if o.dma_sem is not None:
                        ins.then_inc(sems[("dma", o.dma_sem)], 16)
                    elif o.signal:
                        ins.then_inc(sems[("eng", eng_name)], 1)
                if eng_name == "sp":
                    for k in final_wait_sems:
                        handle.wait_ge(sems[("dma", k)], 16 * self.dma_counts[k])
                stats[eng_name] = (len(self.eng_ops[eng_name]), nwait)

            @block.tensor
            def _(t):
                run("pe", t)

            @block.scalar
            def _(s):
                run("act", s)

            @block.vector
            def _(v):
                run("dve", v)

            @block.gpsimd
            def _(g):
                run("pool", g)

            @block.sync
            def _(sp):
                run("sp", sp)
        return stats


class Arena:
    GRAN = 1024

    def __init__(self, nc, es, name, nbytes):
        self.nbytes = nbytes
        self.t = es.enter_context(nc.sbuf_tensor(name, [128, nbytes // 4], F32))
        self.bufs = [Buf(f"{name}_g{i}") for i in range((nbytes + self.GRAN - 1) // self.GRAN)]
        self.cur = 0

    def view(self, off, nbytes, dtype=F32, pattern=None, **kw):
        assert off % 4 == 0 and nbytes % 4 == 0 and off + nbytes <= self.nbytes, (off, nbytes, self.nbytes)
        ap = self.t[:, off // 4:(off + nbytes) // 4]
        if dtype == BF16:
            ap = ap.bitcast(BF16)
        if pattern is not None:
            ap = ap.rearrange(pattern, **kw)
        g0 = off // self.GRAN
        g1 = (off + nbytes + self.GRAN - 1) // self.GRAN
        return View(ap, self.bufs[g0:g1])

    def alloc(self, nbytes, dtype=F32, pattern=None, **kw):
        off = (self.cur + self.GRAN - 1) // self.GRAN * self.GRAN
        self.cur = off + nbytes
        assert self.cur <= self.nbytes, ("arena overflow", self.cur, self.nbytes)
        return self.view(off, nbytes, dtype, pattern, **kw)


class _Stop(Exception):
    pass


def build_program(stop_after=None):
    nc = bass.Bass("TRN2", target_bir_lowering=False)
    S = Sched(nc)

    def ckpt(name):
        if stop_after is not None and name == stop_after:
            raise _Stop()

    def din(name, shape):
        return nc.dram_tensor(name, list(shape), F32, kind="ExternalInput").ap()

    x_d = din("x", [NTOK, D])
    xp_d = din("xp", [NTOK, D])
    mem_d = din("mem", [256, D])
    w_in_d = din("w_in", [D, 2560])
    w_out_d = din("w_out", [D, D])
    w_xq_d = din("w_xq", [D, D])
    w_xk_d = din("w_xk", [D, D])
    w_xv_d = din("w_xv", [D, D])
    w_xo_d = din("w_xo", [D, D])
    w_ff1_d = din("w_ff1", [D, 4096])
    w_ff2_d = din("w_ff2", [4096, D])
    w_pool_d = din("w_pool", [4, 128, 128])
    gains_d = din("gains", [5, D])
    lbl_d = din("lbl", [128, 8])
    gno_d = din("gno", [128, 4])
    psc_d = din("psc", [128, 4])
    icnt_d = din("icnt", [128, 64])
    cst_d = din("cst", [128, 128 + 512 + 512])
    y_d = nc.dram_tensor("y", [NTOK, D], F32, kind="ExternalOutput").ap()

    es = contextlib.ExitStack()
    with es:
        def sb(name, shape, dt):
            return es.enter_context(nc.sbuf_tensor("sb_" + name, list(shape), dt))

        xres = sb("xres", [128, 8, D], F32)
        xbuf = [Buf(f"x{i}") for i in range(8)]
        NSLOT = 7
        wslots = sb("wslots", [128, NSLOT, 8, 512], BF16)
        wslot_buf = [Buf(f"ws{i}") for i in range(NSLOT)]
        wpool = sb("wpool", [128, 4, 128], BF16)
        wpool_b = Buf("wpool")
        gain_t = [sb(f"gain{i}", [128, D], F32) for i in range(2)]
        gain_b = [Buf(f"gain{i}") for i in range(2)]
        scanmask_t = sb("scanmask", [128, 512], BF16)
        ident = sb("ident", [128, 128], BF16)
        ones = sb("ones", [128, 128], BF16)
        ones512 = sb("ones512", [128, 512], BF16)
        mask4 = sb("mask4", [128, 512], BF16)
        cb_b = Buf("constb")
        small = sb("small", [128, 32], F32)
        small_b = Buf("small")
        lbl = sb("lbl", [128, 8], F32)
        gno = sb("gno", [128, 4], F32)
        psc = sb("psc", [128, 4], F32)
        icnt = sb("icnt", [128, 4, 16], F32)
        Sst = sb("Sst", [128, 2, 4, 128], F32)
        S_b = [[Buf(f"S{pp}{h}") for h in range(4)] for pp in range(2)]
        s_cur = [0, 0, 0, 0]
        smid = sb("smid", [128, 4, 2, 128], BF16)
        smid_b = [[Buf(f"smid{h}{j}") for j in range(2)] for h in range(4)]
        stat = sb("stat", [128, 64], F32)
        stat_b = [Buf(f"stat{i}") for i in range(8)]
        mhalf = sb("mhalf", [128, 8], F32)
        hvec = sb("hvec", [128, 4, 48], F32)
        hvec_b = [Buf(f"hvec{h}") for h in range(4)]
        halo = sb("halo", [128, 4, 16], F32)
        halo_b = Buf("halo")
        kT_t = sb("kT", [128, 8, 256], BF16)
        vmem_t = sb("vmem", [128, 2, D], BF16)
        kT = View(kT_t[:], [Buf("kT")])
        vmem = View(vmem_t[:], [Buf("vmem")])

        SCR = 91 * 1024
        ar = Arena(nc, es, "scr", SCR)

        banks_t = [es.enter_context(nc.psum_tensor(f"bank{i}", [128, 512], F32)) for i in range(8)]

        class Bank:
            def __init__(self, i):
                self.f = banks_t[i][:]
                self.bf = banks_t[i][:].bitcast(BF16)
                self.buf = Buf(f"bank{i}")

        banks = [Bank(i) for i in range(8)]
        bank_ctr = [0]

        def next_bank():
            b = banks[bank_ctr[0] % 8]
            bank_ctr[0] += 1
            return b

        hT = [ar.alloc(8192, BF16, "p (k t) -> p k t", k=8) for _ in range(2)]
        HB0 = ar.cur
        hb = [ar.alloc(2048, BF16) for _ in range(4)]
        HB1 = ar.cur
        ar.cur = HB0
        qe = [ar.alloc(1024, BF16) for _ in range(4)]
        sT = [ar.alloc(1024, BF16) for _ in range(4)]
        ke = [ar.alloc(1024, BF16) for _ in range(2)]
        kd = [ar.alloc(1024, BF16) for _ in range(2)]
        kdT = [ar.alloc(1024, BF16, "p (n k) -> p n k", n=4) for _ in range(2)]
        assert ar.cur >= HB1
        junk = hb[3]
        PH = ar.cur
        vtok = ar.alloc(4096, BF16, "p (n c) -> p n c", n=4)
        gate = [ar.alloc(1024, BF16) for _ in range(4)]
        mixedT = ar.alloc(8192, BF16, "p (k t) -> p k t", k=8)
        pbuf = [ar.alloc(4 * 528, F32) for g in range(4)]
        kq_k = [ar.alloc(2048, F32) for _ in range(4)]
        kq_q = [ar.alloc(2048, F32) for _ in range(4)]
        tC = [ar.alloc(4 * 528, F32) for _ in range(2)]
        tD = [ar.alloc(4 * 528, F32) for _ in range(2)]
        sgt = View(tC[1].ap[:, 0:512], tC[1].bufs)
        pooled = [ar.alloc(1024, BF16) for _ in range(4)]
        P1_END = ar.cur
        ar.cur = PH
        memst = [ar.alloc(4096, F32) for _ in range(2)]
        xqT2 = [ar.alloc(8192, BF16, "p (j t) -> p j t", j=8) for _ in range(2)]
        xqT = xqT2[0]
        pT = [ar.alloc(2048, BF16, "p (m t) -> p m t", m=2) for _ in range(4)]
        attT2 = [ar.alloc(8192, BF16, "p (j t) -> p j t", j=8) for _ in range(2)]
        attT = attT2[0]
        rden = [ar.alloc(2048, F32) for _ in range(2)]
        ar.cur = PH
        uT = [ar.alloc(8192, BF16, "p (j t) -> p j t", j=8) for _ in range(2)]
        rtmp = [ar.alloc(2048, F32) for _ in range(3)]
        cst_stage = ar.view(PH + 40 * 1024, 4 * (128 + 512), F32)

        def wsrc_rowsplit(w, c0):
            return w.rearrange("(k p) n -> p k n", p=128)[:, :, c0:c0 + 512]

        units = {}
        for i, nm in enumerate(["in_q", "in_f", "in_i", "in_g", "in_p"]):
            units[nm] = wsrc_rowsplit(w_in_d, 512 * i)
        for nm, w in (("out", w_out_d), ("xq", w_xq_d), ("xk", w_xk_d), ("xv", w_xv_d), ("xo", w_xo_d)):
            for hf in range(2):
                units[f"{nm}{hf}"] = wsrc_rowsplit(w, 512 * hf)
        for j in range(8):
            units[f"ff1_{j}"] = wsrc_rowsplit(w_ff1_d, 512 * j)
        w2v = w_ff2_d.rearrange("(q f p) n -> q p f n", f=8, p=128)
        for q in range(4):
            for hf in range(2):
                units[f"ff2_{q}{hf}"] = w2v[q][:, :, 512 * hf:512 * hf + 512]

        def st_seq(st):
            s = []
            if st == 1:
                s += ["in_f", "in_i", "in_p"]
            s += ["in_q", "in_g", "out0", "out1", "xq0", "xq1", "xo0", "xo1"]
            for q in range(4):
                s += [f"ff1_{2 * q}", f"ff1_{2 * q + 1}", f"ff2_{q}0", f"ff2_{q}1"]
            return [(u, st) for u in s]

        load_seq = [("in_f", 0), ("in_i", 0), ("in_p", 0), ("xk0", 0), ("xk1", 0), ("xv0", 0), ("xv1", 0)]
        load_seq += st_seq(0) + st_seq(1)
        load_pos = [0]
        free_slots = list(range(NSLOT))
        loaded = {}

        def pump(startup=False):
            while free_slots and load_pos[0] < len(load_seq):
                key = load_seq[load_pos[0]]
                idx = load_pos[0]
                load_pos[0] += 1
                sl = free_slots.pop(0)
                loaded[key] = sl
                extra = []
                if startup and idx >= 2:
                    extra = xbuf[0:4] if idx < 4 else xbuf[0:8]
                S.dma("pool", wslots[:, sl], units[key[0]], f"w{sl}", reads=extra, writes=[wslot_buf[sl]],
                      name=f"ld_{key}")

        def W(name, st):
            key = (name, st)
            assert key in loaded, f"weight unit {key} not loaded yet"
            sl = loaded[key]
            return wslots[:, sl], wslot_buf[sl]

        def release(name, st):
            sl = loaded.pop((name, st))
            free_slots.append(sl)
            pump()

        for i_ in range(4):
            S.dma("sp", xres[:, i_, :], xp_d[128 * i_:128 * i_ + 128, :], f"x{i_}", writes=[xbuf[i_]])
        S.dma("sp", cst_stage.ap, cst_d[:, 0:640], "c0", writes=[cst_stage])
        S.dma("pool", scanmask_t[:], cst_d[:, 640:1152], "c0b", writes=[cb_b])
        vec_b = [Buf(f"vec{i}") for i in range(4)]
        S.dma("sp", lbl[:], lbl_d[:, :], "c1", writes=[vec_b[0]])
        S.dma("sp", gno[:], gno_d[:, :], "c2", writes=[vec_b[1]])
        S.dma("sp", psc[:], psc_d[:, :], "c3", writes=[vec_b[2]])
        S.dma("sp", icnt[:].rearrange("p g t -> p (g t)"), icnt_d[:, :], "c4", writes=[vec_b[3]])
        S.dma("pool", wpool[:], w_pool_d.rearrange("g c d -> c g d"), "c5", writes=[wpool_b])

        def load_gain(slot, row):
            S.dma("sp", gain_t[slot][:], gains_d[row:row + 1, :].partition_broadcast(128), f"g{slot}",
                  writes=[gain_b[slot]])

        G_MIX, G_X, G_MEM, G_FFN, G_FIN = 0, 1, 2, 3, 4
        load_gain(0, G_MIX)
        load_gain(1, G_MEM)

        S.op("dve", lambda e: e.tensor_copy(ident[:], cst_stage.ap[:, 0:128]), reads=[cst_stage], writes=[cb_b])
        S.op("dve", lambda e: e.tensor_copy(mask4[:], cst_stage.ap[:, 128:640]), reads=[cst_stage], writes=[cb_b])
        S.op("pool", lambda e: e.memset(ones[:], 1.0), writes=[cb_b])
        S.op("pool", lambda e: e.memset(ones512[:], 1.0), writes=[cb_b])
        S.op("pool", lambda e: e.memset(mhalf[:], -0.5), writes=[small_b])
        S.op("pool", lambda e: e.memset(Sst[:], 0.0), writes=S_b)
        S.op("dve", lambda e: e.tensor_tensor(small[:, 16:20], lbl[:, 0:4], lbl[:, 4:8], ALU.subtract),
             reads=[vec_b], writes=[small_b])
        S.op("act", lambda e: e.activation(out=small[:, 0:4], in_=small[:, 16:20], func=AF.Sigmoid),
             reads=[small_b], writes=[small_b])
        S.op("dve", lambda e: e.tensor_scalar(small[:, 4:8], small[:, 0:4], -1.0, 1.0, ALU.mult, ALU.add),
             reads=[small_b], writes=[small_b])
        S.op("dve", lambda e: e.tensor_scalar(small[:, 8:12], small[:, 0:4], 1.0, -1.0, ALU.mult, ALU.add),
             reads=[small_b], writes=[small_b])
        oml_c = lambda h: small[:, 4 + h:5 + h]
        noml_c = lambda h: small[:, 8 + h:9 + h]

        stat_ctr = [0]
        evac_ctr = [0]

        def evac_copy(dst_ap, src_ap, reads, writes, eng=None):
            if eng is None:
                eng = "act" if evac_ctr[0] % 2 == 0 else "dve"
                evac_ctr[0] += 1
            if eng == "act":
                S.op("act", lambda e: e.copy(dst_ap, src_ap), reads=reads, writes=writes)
            else:
                S.op("dve", lambda e: e.tensor_copy(dst_ap, src_ap), reads=reads, writes=writes)

        def norm_stats(src_blocks):
            si = stat_ctr[0] % 8
            stat_ctr[0] += 1
            st_ap = stat[:, 8 * si:8 * si + 8]
            stb = stat_b[si]
            nb = len(src_blocks)
            for i, (xap, xb) in enumerate(src_blocks):
                S.op("act", lambda e, xap=xap, i=i: e.activation(out=hb[i].ap, in_=xap, func=AF.Square,
                                                               accum_out=st_ap[:, i:i + 1]),
                     reads=[xb], writes=[hb[i], stb])
            S.op("act", lambda e: e.activation(out=st_ap[:, 0:nb], in_=st_ap[:, 0:nb], func=AF.Ln, scale=1.0 / D, bias=EPS),
                 reads=[stb], writes=[stb])
            S.op("act", lambda e: e.activation(out=st_ap[:, 4:4 + nb], in_=st_ap[:, 0:nb], func=AF.Exp, scale=-0.5),
                 reads=[stb], writes=[stb])
            return (lambda i: st_ap[:, 4 + i:5 + i]), stb

        def norm_elem(src_blocks, gslot):
            rs, stb = norm_stats(src_blocks)
            for i, (xap, xb) in enumerate(src_blocks):
                S.op("dve", lambda e, xap=xap, i=i: e.scalar_tensor_tensor(
                    hb[i].ap, xap, rs(i), gain_t[gslot][:], ALU.mult, ALU.mult),
                    reads=[xb, stb, gain_b[gslot]], writes=[hb[i]])

        def norm_T(nb, hT_dst):
            w = nb * 128
            for kp in range(4):
                bk = next_bank()
                for kk in range(2):
                    kc = 2 * kp + kk
                    for i in range(nb):
                        S.op("pe", lambda e, bk=bk, kk=kk, i=i, kc=kc: e.transpose(
                            bk.bf[:, kk * 512 + i * 128: kk * 512 + (i + 1) * 128],
                            hb[i].ap[:, kc * 128:(kc + 1) * 128], ident[:]),
                            reads=[hb[i], cb_b], writes=[bk.buf])
                for kk in range(2):
                    kc = 2 * kp + kk
                    evac_copy(hT_dst.ap[:, kc, 0:w], bk.bf[:, kk * 512:kk * 512 + w], [], [bk.buf, hT_dst.bufs[kc]],
                              eng=("act" if kk == 0 else "dve"))

        def norm_tile(src_blocks, gslot, hT_dst):
            norm_elem(src_blocks, gslot)
            norm_T(len(src_blocks), hT_dst)

        class Hooks:
            def __init__(self, a=None, b=None):
                self.a = a
                self.b = b

            def run_a(self):
                if self.a is not None:
                    self.a()
                    self.a = None

            def run_b(self):
                if self.b is not None:
                    self.b()
                    self.b = None

        def norm_hooks(src_blocks, gslot, hT_dst):
            return Hooks(lambda: norm_elem(src_blocks, gslot), lambda: norm_T(len(src_blocks), hT_dst))

        NOHOOK = Hooks()

        def proj_B(unit_ap, unit_buf, mchunk, hT_src, ncols=512, col0=0):
            bk = next_bank()
            for kc in range(8):
                S.op("pe", lambda e, bk=bk, kc=kc: e.matmul(
                    bk.f[:, 0:ncols], unit_ap[:, kc, 128 * mchunk:128 * mchunk + 128],
                    hT_src.ap[:, kc, col0:col0 + ncols], start=(kc == 0), stop=(kc == 7)),
                    reads=[unit_buf, hT_src.bufs[kc]], writes=[bk.buf])
            return bk

        def proj_A(lhs_view, blk, unit_ap, unit_buf, order=(0, 1, 2, 3, 4, 5, 6, 7)):
            bk = next_bank()
            for idx, j in enumerate(order):
                S.op("pe", lambda e, bk=bk, j=j, idx=idx: e.matmul(
                    bk.f[:, :], lhs_view.ap[:, j, 128 * blk:128 * blk + 128], unit_ap[:, j, :],
                    start=(idx == 0), stop=(idx == 7)),
                    reads=[unit_buf, lhs_view.bufs[j]], writes=[bk.buf])
            return bk

        def resid_add(bk, blk, hf):
            xa = xres[:, blk, 512 * hf:512 * hf + 512]
            S.op("dve", lambda e: e.tensor_tensor(xa, xa, bk.f[:, :], ALU.add),
                 reads=[xbuf[blk]], writes=[xbuf[blk], bk.buf])

        def load_x_block(src_d, row0, blk):
            S.dma("sp", xres[:, blk, :], src_d[row0:row0 + 128, :], f"x{blk}", writes=[xbuf[blk]])

        class HC:
            pass

        def hg_ln(c):
            S.op("act", lambda e: e.activation(out=c.Cv, in_=kq_k[c.h].ap, func=AF.Ln, scale=-1.0, bias=1.0),
                 reads=[kq_k[c.h]], writes=[c.tC])

        def hg_scan(c):
            S.op("dve", lambda e: e.tensor_tensor_scan(out=c.Dv, data0=scanmask_t[:], data1=c.Cv, initial=0.0,
                                                       op0=ALU.mult, op1=ALU.add),
                 reads=[c.tC, cb_b], writes=[c.tD])

        def hg_extract(c, main):
            hv, d3 = c.hv, c.d3
            if main:
                S.op("dve", lambda e: e.tensor_copy(hv[:, 0:16].rearrange("p (k c) -> p c k", k=2), d3[:, :, 31::32]),
                     reads=[c.tD], writes=[c.hvb])
                S.op("dve", lambda e: e.tensor_tensor(hv[:, 16:24], hv[:, 8:16], hv[:, 0:8], ALU.subtract),
                     reads=[c.hvb], writes=[c.hvb])
            else:
                S.op("dve", lambda e: e.tensor_copy(hv[:, 8:16], d3[:, :, 63]), reads=[c.tD], writes=[c.hvb])
                S.op("dve", lambda e: e.tensor_copy(hv[:, 0:8], d3[:, :, 63]), reads=[c.tD], writes=[c.hvb])

        def hg_exp_small(c, main):
            hv = c.hv
            if main:
                S.op("act", lambda e: e.activation(out=hv[:, 24:48], in_=hv[:, 0:24], func=AF.Exp),
                     reads=[c.hvb], writes=[c.hvb])
            else:
                S.op("act", lambda e: e.activation(out=hv[:, 8:16], in_=hv[:, 8:16], func=AF.Exp),
                     reads=[c.hvb], writes=[c.hvb])

        def hg_sub(c, main):
            col = 0
            S.op("dve", lambda e: e.tensor_tensor(
                c.d3, c.d3, c.hv[:, col:col + 8].rearrange("p (c o) -> p c o", o=1).to_broadcast([128, 8, 64]),
                ALU.subtract), reads=[c.tD, c.hvb], writes=[c.tD])

        def hg_exp_big(c, main):
            if main:
                S.op("act", lambda e: e.activation(out=c.Cv, in_=c.Dv, func=AF.Exp), reads=[c.tD], writes=[c.tC])
            S.op("act", lambda e: e.activation(out=c.Dv, in_=c.Dv, func=AF.Exp, scale=-1.0), reads=[c.tD], writes=[c.tD])

        def hg_mults(c, main):
            h, r = c.h, c.r
            if main:
                S.op("dve", lambda e: e.tensor_tensor(qe[h].ap, kq_q[h].ap, c.Cv, ALU.mult),
                     reads=[kq_q[h], c.tC], writes=[qe[h]])
                S.op("dve", lambda e: e.tensor_tensor(ke[r].ap, kq_k[h].ap, c.Dv, ALU.mult),
                     reads=[kq_k[h], c.tD], writes=[ke[r]])
                ke3 = ke[r].ap.rearrange("p (c j) -> p c j", j=64)
                kd3 = kd[r].ap.rearrange("p (c j) -> p c j", j=64)
                S.op("pool", lambda e: e.tensor_tensor(
                    kd3, ke3, c.hv[:, 40:48].rearrange("p (c o) -> p c o", o=1).to_broadcast([128, 8, 64]), ALU.mult),
                    reads=[ke[r], c.hvb], writes=[kd[r]])
            else:
                S.op("dve", lambda e: e.tensor_tensor(kd[r].ap, kq_k[h].ap, c.Dv, ALU.mult),
                     reads=[kq_k[h], c.tD], writes=[kd[r]])

        def hg_pe_front(c, main):
            h, r = c.h, c.r
            if main:
                sb_ = next_bank()
                for n in range(4):
                    S.op("pe", lambda e, n=n: e.matmul(
                        sb_.f[:, 128 * n:128 * n + 128], ke[r].ap[:, 128 * n:128 * n + 128],
                        qe[h].ap[:, 128 * n:128 * n + 128], start=True, stop=True),
                        reads=[ke[r], qe[h]], writes=[sb_.buf])
                S.op("dve", lambda e: e.tensor_tensor(sT[h].ap, sb_.f[:, :], mask4[:], ALU.mult),
                     reads=[cb_b], writes=[sb_.buf, sT[h]])
            bk = next_bank()
            for n in range(4):
                S.op("pe", lambda e, n=n: e.transpose(bk.bf[:, 128 * n:128 * n + 128],
                                                      kd[r].ap[:, 128 * n:128 * n + 128], ident[:]),
                     reads=[kd[r], cb_b], writes=[bk.buf])
            evac_copy(kdT[r].ap.rearrange("p n k -> p (n k)"), bk.bf[:, 0:512], [], [bk.buf, kdT[r]], eng="act")

        def hg_pe_state(c):
            h, r = c.h, c.r
            c.pbanks = [next_bank(), next_bank()]
            for ch in range(8):
                n, half = divmod(ch, 2)
                pb = c.pbanks[half]
                S.op("pe", lambda e, pb=pb, n=n, half=half: e.matmul(
                    pb.f[:, n * 128:n * 128 + 128],
                    kdT[r].ap[64 * half:64 * half + 64, n, :],
                    vtok.ap[64 * half:64 * half + 64, n, 128 * h:128 * h + 128], start=True, stop=True),
                    reads=[kdT[r], vtok.bufs[n]], writes=[pb.buf])

        def hg_state_step(c, ch):
            h = c.h
            cur = s_cur[h]
            nxt = 1 - cur
            pb = c.pbanks[ch % 2]
            S.op("dve", lambda e: e.scalar_tensor_tensor(
                Sst[:, nxt, h, :], Sst[:, cur, h, :], c.hv[:, 32 + ch:33 + ch], pb.f[:, (ch // 2) * 128:(ch // 2) * 128 + 128],
                ALU.mult, ALU.add),
                reads=[S_b[cur][h], c.hvb], writes=[S_b[nxt][h], pb.buf])
            s_cur[h] = nxt

        def hg_smid(c, ch):
            h = c.h
            cur = s_cur[h]
            j = ch % 2
            S.op("pool", lambda e: e.tensor_scalar(smid[:, h, j, :], Sst[:, cur, h, :], c.hv[:, 24 + ch:25 + ch], 0.0,
                                                   ALU.mult, ALU.add),
                 reads=[S_b[cur][h], c.hvb], writes=[smid_b[h][j]])

        def pf_front_stages(cs):
            def st_ln():
                for c in cs:
                    hg_ln(c)

            def st_scan():
                for c in cs:
                    S.op("dve", lambda e, c=c: e.tensor_tensor_scan(out=c.Dv, data0=ones512[:], data1=c.Cv, initial=0.0,
                                                                op0=ALU.mult, op1=ALU.add),
                         reads=[c.tC, cb_b], writes=[c.tD])

            def st_extract():
                for c in cs:
                    S.op("dve", lambda e, c=c: e.tensor_copy(c.hv[:, 0:1], c.Dv[:, 511:512]), reads=[c.tD], writes=[c.hvb])

            def st_exp_small():
                for c in cs:
                    S.op("act", lambda e, c=c: e.activation(out=c.hv[:, 8:9], in_=c.hv[:, 0:1], func=AF.Exp),
                         reads=[c.hvb], writes=[c.hvb])

            def st_sub():
                for c in cs:
                    S.op("dve", lambda e, c=c: e.tensor_scalar(c.Dv, c.Dv, c.hv[:, 0:1], None, ALU.subtract),
                         reads=[c.tD, c.hvb], writes=[c.tD])

            def st_exp_big():
                for c in cs:
                    hg_exp_big(c, False)

            def st_mults():
                for c in cs:
                    hg_mults(c, False)
            return [st_ln, st_scan, st_extract, st_exp_small, st_sub, st_exp_big, st_mults]

        def pf_pe(cs):
            for c in cs:
                hg_pe_front(c, False)
            for c in cs:
                h, r = c.h, c.r
                c.pb = next_bank()
                for n in range(4):
                    S.op("pe", lambda e, c=c, n=n, h=h, r=r: e.matmul(
                        c.pb.f[:, 0:128], kdT[r].ap[:, n, :], vtok.ap[:, n, 128 * h:128 * h + 128],
                        start=(n == 0), stop=(n == 3)),
                        reads=[kdT[r], vtok.bufs[n]], writes=[c.pb.buf])

        def pf_step(cs):
            for c in cs:
                h = c.h
                cur = s_cur[h]
                nxt = 1 - cur
                S.op("dve", lambda e, c=c, h=h, cur=cur, nxt=nxt: e.scalar_tensor_tensor(
                    Sst[:, nxt, h, :], Sst[:, cur, h, :], c.hv[:, 8:9], c.pb.f[:, 0:128], ALU.mult, ALU.add),
                    reads=[S_b[cur][h], c.hvb], writes=[S_b[nxt][h], c.pb.buf])
                s_cur[h] = nxt

        def make_hc(h):
            c = HC()
            c.h = h
            c.r = h % 2
            c.hv = hvec[:, h, :]
            c.hvb = hvec_b[h]
            c.tC = tC[c.r]
            c.tD = tD[c.r]
            c.Cv = tC[c.r].ap[:, 0:512]
            c.Dv = tD[c.r].ap[:, 0:512]
            c.d3 = c.Dv.rearrange("p (c j) -> p c j", j=64)
            return c

        out_sems = set()
        try:
            def prefix_norm(q):
                blks = [4 * (q % 2) + i for i in range(4)]
                norm_tile([(xres[:, b, :], xbuf[b]) for b in blks], 0, hT[q % 2])

            for i in range(4):
                load_x_block(xp_d, 512 + 128 * i, 4 + i)
            pump(startup=True)
            prefix_norm(0)
            for q in range(4):
                hTq = hT[q % 2]
                if q + 1 < 4:
                    nblks = [4 * ((q + 1) % 2) + i for i in range(4)]
                    norm_elem([(xres[:, b, :], xbuf[b]) for b in nblks], 0)
                for i in range(4):
                    blk = 4 * (q % 2) + i
                    if q + 2 < 4:
                        load_x_block(xp_d, 512 * (q + 2) + 128 * i, blk)
                    else:
                        load_x_block(x_d, 128 * blk, blk)
                uf, ufb = W("in_f", 0)
                ui, uib = W("in_i", 0)
                for h in range(4):
                    bk = proj_B(uf, ufb, h, hTq)
                    S.op("act", lambda e, bk=bk, h=h: e.activation(out=kq_k[h].ap, in_=bk.f[:, :], func=AF.Sigmoid),
                         writes=[bk.buf, kq_k[h]])
                    S.op("dve", lambda e, h=h: e.tensor_scalar(kq_k[h].ap, kq_k[h].ap, noml_c(h), oml_c(h), ALU.mult, ALU.add),
                         reads=[kq_k[h], small_b], writes=[kq_k[h]])
                cs0 = [make_hc(0), make_hc(1)]
                cs1 = [make_hc(2), make_hc(3)]
                PF0 = pf_front_stages(cs0)
                PF0[0]()
                PF0[1]()
                for n in range(4):
                    bk = proj_A(hTq, n, ui, uib)
                    evac_copy(vtok.ap[:, n, :], bk.f[:, :], [], [bk.buf, vtok.bufs[n]])
                if q + 1 < 4:
                    norm_T(4, hT[(q + 1) % 2])
                if q == 3:
                    up, upb = W("in_p", 0)
                    for g in range(4):
                        bk = proj_B(up, upb, g, hTq, ncols=16, col0=496)
                        S.op("act", lambda e, bk=bk, g=g: e.copy(halo[:, g, :], bk.f[:, 0:16]),
                             writes=[bk.buf, halo_b])
                for th in PF0[2:]:
                    th()
                pf_pe(cs0)
                for th in pf_front_stages(cs1):
                    th()
                pf_step(cs0)
                pf_pe(cs1)
                pf_step(cs1)
                ckpt(f"prefix{q}")

            for mb in range(2):
                S.dma("sp", memst[mb].ap, mem_d[128 * mb:128 * mb + 128, :], f"m{mb}", writes=[memst[mb]])
            norm_tile([(memst[mb].ap, memst[mb]) for mb in range(2)], 1, hT[0])
            hmT = hT[0]
            for uh in range(2):
                uk, ukb = W(f"xk{uh}", 0)
                for jp in range(2):
                    bk = next_bank()
                    for jj in range(2):
                        j4 = 2 * jp + jj
                        for kc in range(8):
                            S.op("pe", lambda e, bk=bk, kc=kc, jj=jj, j4=j4, uk=uk: e.matmul(
                                bk.f[:, 256 * jj:256 * jj + 256], uk[:, kc, 128 * j4:128 * j4 + 128], hmT.ap[:, kc, 0:256],
                                start=(kc == 0), stop=(kc == 7)),
                                reads=[ukb, hmT], writes=[bk.buf])
                    j0 = 4 * uh + 2 * jp
                    evac_copy(kT.ap[:, j0:j0 + 2, :].rearrange("p j m -> p (j m)"), bk.f[:, :], [], [bk.buf, kT])
            for mb in range(2):
                for hf in range(2):
                    uv, uvb = W(f"xv{hf}", 0)
                    bk = proj_A(hmT, mb, uv, uvb)
                    evac_copy(vmem.ap[:, mb, 512 * hf:512 * hf + 512], bk.f[:, :], [], [bk.buf, vmem])
            for nm in ("xk0", "xk1", "xv0", "xv1"):
                release(nm, 0)
            load_gain(1, G_X)
            ckpt("kv")


            def mixer_tile(st, t, hk=NOHOOK, late=False):
                hTt = hT[t]
                blks = [4 * t + i for i in range(4)]
                uq, uqb = W("in_q", st)
                uf, ufb = W("in_f", st)
                ui, uib = W("in_i", st)
                ug, ugb = W("in_g", st)
                up, upb = W("in_p", st)
                for g in range(4):
                    bk = proj_B(up, upb, g, hTt)
                    S.op("act", lambda e, bk=bk, g=g: e.copy(pbuf[g].ap[:, 16:528], bk.f[:, :]),
                         writes=[bk.buf, pbuf[g]])
                    S.op("pool", lambda e, g=g: e.tensor_copy(pbuf[g].ap[:, 0:16], halo[:, g, :]),
                         reads=[halo_b], writes=[pbuf[g]])
                for g in range(4):
                    wlen = POOL_W[g]
                    cur = pbuf[g].ap
                    curv = pbuf[g]
                    sh = 1
                    tmps = [tC[0], tD[0]]
                    ti = 0
                    while sh < wlen:
                        dst = tmps[ti % 2]
                        lo = 2 * sh - 1
                        S.op("pool", lambda e, dst=dst, cur=cur, sh=sh, lo=lo: e.tensor_tensor(
                            dst.ap[:, lo:528], cur[:, lo:528], cur[:, lo - sh:528 - sh], ALU.add),
                            reads=[curv], writes=[dst])
                        cur = dst.ap
                        curv = dst
                        ti += 1
                        sh *= 2
                    if t == 0 and st == 0:
                        S.op("pool", lambda e, g=g, cur=cur: e.tensor_tensor(
                            cur[:, 16:32], cur[:, 16:32], icnt[:, g, :], ALU.mult),
                            reads=[curv, vec_b], writes=[curv])
                        S.op("pool", lambda e, g=g, cur=cur, wlen=wlen: e.tensor_scalar(
                            cur[:, 32:528], cur[:, 32:528], 1.0 / wlen, 0.0, ALU.mult, ALU.add),
                            reads=[curv], writes=[curv])
                    else:
                        S.op("pool", lambda e, g=g, cur=cur, wlen=wlen: e.tensor_scalar(
                            cur[:, 16:528], cur[:, 16:528], 1.0 / wlen, 0.0, ALU.mult, ALU.add),
                            reads=[curv], writes=[curv])
                    S.op("pool", lambda e, g=g, cur=cur: e.tensor_tensor(
                        pooled[g].ap, cur[:, 16:528], pbuf[g].ap[:, 16:528], ALU.subtract),
                        reads=[curv, pbuf[g]], writes=[pooled[g]])
                    S.op("pool", lambda e, g=g: e.tensor_copy(halo[:, g, :], pbuf[g].ap[:, 512:528]),
                         reads=[pbuf[g]], writes=[halo_b])
                if not late:
                    hk.run_a()
                for h in range(4):
                    bk = proj_B(uf, ufb, h, hTt)
                    S.op("act", lambda e, bk=bk, h=h: e.activation(out=kq_k[h].ap, in_=bk.f[:, :], func=AF.Sigmoid),
                         writes=[bk.buf, kq_k[h]])
                    S.op("dve", lambda e, h=h: e.tensor_scalar(kq_k[h].ap, kq_k[h].ap, noml_c(h), oml_c(h), ALU.mult, ALU.add),
                         reads=[kq_k[h], small_b], writes=[kq_k[h]])
                for h in range(4):
                    bk = proj_B(uq, uqb, h, hTt)
                    S.op("act", lambda e, bk=bk: e.activation(out=sgt.ap, in_=bk.f[:, :], func=AF.Sigmoid),
                         writes=[bk.buf, sgt])
                    S.op("dve", lambda e, bk=bk, h=h: e.tensor_tensor(kq_q[h].ap, sgt.ap, bk.f[:, :], ALU.mult),
                         reads=[sgt], writes=[bk.buf, kq_q[h]])
                if late:
                    hk.run_a()
                else:
                    hk.run_b()
                def do_poolmm():
                    for g in range(4):
                        bk = next_bank()
                        S.op("pe", lambda e, bk=bk, g=g: e.matmul(bk.f[:, :], wpool[:, g, :], pooled[g].ap, start=True, stop=True),
                             reads=[wpool_b, pooled[g]], writes=[bk.buf])
                        S.op("act", lambda e, bk=bk, g=g: e.activation(out=mixedT.ap[:, 4 + g, :], in_=bk.f[:, :], func=AF.Copy,
                                                                     scale=psc[:, g:g + 1]),
                             reads=[vec_b], writes=[bk.buf, mixedT.bufs[4 + g]])
                def do_g():
                    for h in range(4):
                        bk = proj_B(ug, ugb, h, hTt)
                        S.op("act", lambda e, bk=bk: e.activation(out=sgt.ap, in_=bk.f[:, :], func=AF.Sigmoid),
                             writes=[bk.buf, sgt])
                        S.op("dve", lambda e, bk=bk, h=h: e.tensor_tensor(gate[h].ap, sgt.ap, bk.f[:, :], ALU.mult),
                             reads=[sgt], writes=[bk.buf, gate[h]])
                def do_v():
                    for n in range(4):
                        bk = proj_A(hTt, n, ui, uib)
                        evac_copy(vtok.ap[:, n, :], bk.f[:, :], [], [bk.buf, vtok.bufs[n]])
                def front_stages(cs):
                    L = []
                    for stage in (hg_ln, hg_scan):
                        L.append(lambda stage=stage: [stage(c) for c in cs])
                    for stage in (hg_extract, hg_exp_small, hg_sub, hg_exp_big, hg_mults):
                        L.append(lambda stage=stage: [stage(c, True) for c in cs])
                    return L

                def pe_part(cs):
                    for c in cs:
                        hg_pe_front(c, True)
                    for c in cs:
                        hg_pe_state(c)

                def chain_blocks(cs):
                    for c in cs:
                        c.ob = next_bank()

                    def blk(n):
                        for c in cs:
                            h, r = c.h, c.r
                            S.op("pe", lambda e, c=c, n=n, h=h, r=r: e.matmul(
                                c.ob.f[:, 128 * n:128 * n + 128], vtok.ap[:, n, 128 * h:128 * h + 128],
                                sT[h].ap[:, 128 * n:128 * n + 128], start=True, stop=False),
                                reads=[vtok.bufs[n], sT[h]], writes=[c.ob.buf])
                        for j in range(2):
                            ch = 2 * n + j
                            for c in cs:
                                h, r = c.h, c.r
                                hg_smid(c, ch)
                                S.op("pe", lambda e, c=c, n=n, j=j, h=h, r=r: e.matmul(
                                    c.ob.f[:, 128 * n + 64 * j:128 * n + 64 * j + 64], smid[:, h, j, :],
                                    qe[h].ap[:, 128 * n + 64 * j:128 * n + 64 * j + 64], start=False, stop=(j == 1)),
                                    reads=[smid_b[h][j], qe[h]], writes=[c.ob.buf])
                                hg_state_step(c, ch)
                    return [lambda n=n: blk(n) for n in range(4)]

                def onorm_steps(cs):
                    def s1():
                        for c in cs:
                            S.op("act", lambda e, c=c: e.activation(out=sT[c.h].ap, in_=c.ob.f[:, :], func=AF.Square),
                                 writes=[c.ob.buf, sT[c.h]])
                        for c in cs:
                            c.nb = next_bank()
                            S.op("pe", lambda e, c=c: e.matmul(c.nb.f[:, :], ones[:], sT[c.h].ap, start=True, stop=True),
                                 reads=[cb_b, sT[c.h]], writes=[c.nb.buf])

                    def s2():
                        for c in cs:
                            S.op("act", lambda e, c=c: e.activation(out=kq_k[c.h].ap, in_=c.nb.f[:, :], func=AF.Ln, scale=1.0 / 128, bias=EPS),
                                 writes=[c.nb.buf, kq_k[c.h]])
                        for c in cs:
                            S.op("act", lambda e, c=c: e.activation(out=kq_k[c.h].ap, in_=kq_k[c.h].ap, func=AF.Exp, scale=-0.5),
                                 reads=[kq_k[c.h]], writes=[kq_k[c.h]])

                    def s3():
                        for c in cs:
                            S.op("dve", lambda e, c=c: e.tensor_tensor(kq_q[c.h].ap, kq_k[c.h].ap, c.ob.f[:, :], ALU.mult),
                                 reads=[kq_k[c.h]], writes=[c.ob.buf, kq_q[c.h]])
                        for c in cs:
                            S.op("dve", lambda e, c=c: e.scalar_tensor_tensor(
                                mixedT.ap[:, c.h, :], kq_q[c.h].ap, gno[:, c.h:c.h + 1], gate[c.h].ap, ALU.mult, ALU.mult),
                                reads=[kq_q[c.h], vec_b, gate[c.h]], writes=[mixedT.bufs[c.h]])
                    return [s1, s2, s3]

                def weave(A, B):
                    A = list(A)
                    B = list(B)
                    while A or B:
                        if A:
                            A.pop(0)()
                        if B:
                            B.pop(0)()

                cs0 = [make_hc(0), make_hc(1)]
                cs1 = [make_hc(2), make_hc(3)]
                F0 = front_stages(cs0)
                do_g()
                F0[0]()
                F0[1]()
                do_v()
                hk.run_b()
                for th_ in F0[2:]:
                    th_()
                do_poolmm()
                pe_part(cs0)
                weave(front_stages(cs1), chain_blocks(cs0) + onorm_steps(cs0))
                pe_part(cs1)
                for th in chain_blocks(cs1) + onorm_steps(cs1):
                    th()

            def mixer_out(st, t):
                for n in range(4):
                    for hf in range(2):
                        uo, uob = W(f"out{hf}", st)
                        bk = proj_A(mixedT, n, uo, uob, order=(4, 5, 6, 7, 0, 1, 2, 3))
                        resid_add(bk, 4 * t + n, hf)

            def xq_groups(st, t):
                hTt = hT[t]
                xq_ = xqT2[t]

                def grp(j):
                    uq, uqb = W(f"xq{j // 4}", st)
                    bk = proj_B(uq, uqb, j % 4, hTt)
                    evac_copy(xq_.ap[:, j, :], bk.f[:, :], [], [bk.buf, xq_.bufs[j]])
                return [lambda j=j: grp(j) for j in range(8)]

            def core_steps(st, t):
                xq_ = xqT2[t]
                at_ = attT2[t]

                def scores_exp(h):
                    for mc in range(2):
                        bk = next_bank()
                        for ec in range(2):
                            j = 2 * h + ec
                            S.op("pe", lambda e, bk=bk, mc=mc, j=j, ec=ec: e.matmul(
                                bk.f[:, :], kT.ap[:, j, 128 * mc:128 * mc + 128], xq_.ap[:, j, :],
                                start=(ec == 0), stop=(ec == 1)),
                                reads=[kT, xq_.bufs[j]], writes=[bk.buf])
                        S.op("act", lambda e, bk=bk, h=h, mc=mc: e.activation(out=pT[h].ap[:, mc, :], in_=bk.f[:, :],
                                                                           func=AF.Exp, scale=1.0 / 16),
                             writes=[bk.buf, pT[h].bufs[mc]])

                def den_att(h):
                    db = next_bank()
                    for mc in range(2):
                        S.op("pe", lambda e, db=db, h=h, mc=mc: e.matmul(db.f[:, :], ones[:], pT[h].ap[:, mc, :],
                                                                      start=(mc == 0), stop=(mc == 1)),
                             reads=[cb_b, pT[h].bufs[mc]], writes=[db.buf])
                    rd = rden[h % 2]
                    S.op("act", lambda e, db=db, rd=rd: e.activation(out=rd.ap, in_=db.f[:, :], func=AF.Ln),
                         writes=[db.buf, rd])
                    S.op("act", lambda e, rd=rd: e.activation(out=rd.ap, in_=rd.ap, func=AF.Exp, scale=-1.0),
                         reads=[rd], writes=[rd])
                    for ec in range(2):
                        j = 2 * h + ec
                        bk = next_bank()
                        for mc in range(2):
                            S.op("pe", lambda e, bk=bk, mc=mc, j=j, h=h: e.matmul(
                                bk.f[:, :], vmem.ap[:, mc, 128 * j:128 * j + 128], pT[h].ap[:, mc, :],
                                start=(mc == 0), stop=(mc == 1)),
                                reads=[vmem, pT[h].bufs[mc]], writes=[bk.buf])
                        S.op("dve", lambda e, bk=bk, j=j, rd=rd: e.tensor_tensor(at_.ap[:, j, :], rd.ap, bk.f[:, :], ALU.mult),
                             reads=[rd], writes=[bk.buf, at_.bufs[j]])

                return [lambda: scores_exp(0), lambda: scores_exp(1),
                        lambda: (scores_exp(2), den_att(0)), lambda: (scores_exp(3), den_att(1)),
                        lambda: den_att(2), lambda: den_att(3)]

            def xo_groups(st, t):
                at_ = attT2[t]

                def grp(n, hf):
                    uo, uob = W(f"xo{hf}", st)
                    bk = proj_A(at_, n, uo, uob)
                    resid_add(bk, 4 * t + n, hf)
                return [lambda n=n, hf=hf: grp(n, hf) for n in range(4) for hf in range(2)]

            def weave2(A, B):
                A = list(A)
                B = list(B)
                while A or B:
                    if A:
                        A.pop(0)()
                    if B:
                        B.pop(0)()

            ffn_ctr = [0]

            def ffn_ff1(st, q, t, hk=NOHOOK):
                hTt = hT[t]
                hk.run_a()
                u = uT[ffn_ctr[0] % 2]
                ffn_ctr[0] += 1
                for fc in range(8):
                    u1, u1b = W(f"ff1_{2 * q + fc // 4}", st)
                    bk = proj_B(u1, u1b, fc % 4, hTt)
                    rt = rtmp[fc % 3]
                    S.op("act", lambda e, bk=bk, rt=rt: e.activation(out=rt.ap, in_=bk.f[:, :], func=AF.Relu),
                         writes=[bk.buf, rt])
                    S.op("pool", lambda e, rt=rt, fc=fc, u=u: e.tensor_tensor(u.ap[:, fc, :], rt.ap, rt.ap, ALU.mult),
                         reads=[rt], writes=[u.bufs[fc]])
                    if fc == 6:
                        hk.run_b()
                return u

            def ffn_ff2(st, q, t, u, hk=NOHOOK):
                hk.run_a()
                for n in range(4):
                    for hf in range(2):
                        u2, u2b = W(f"ff2_{q}{hf}", st)
                        bk = proj_A(u, n, u2, u2b)
                        resid_add(bk, 4 * t + n, hf)
                    if n == 1:
                        hk.run_b()

            def final_tile(st, t, gslot):
                blks = [4 * t + i for i in range(4)]
                rs, stb = norm_stats([(xres[:, b, :], xbuf[b]) for b in blks])
                for i, b in enumerate(blks):
                    S.op("dve", lambda e, b=b, i=i: e.scalar_tensor_tensor(
                        xres[:, b, :], xres[:, b, :], rs(i), gain_t[gslot][:], ALU.mult, ALU.mult),
                        reads=[xbuf[b], stb, gain_b[gslot]], writes=[xbuf[b]])
                    row0 = 1024 * st + 128 * b
                    S.dma("sp", y_d[row0:row0 + 128, :], xres[:, b, :], f"x{b}", reads=[xbuf[b]])
                    out_sems.add(f"x{b}")

            def x_blocks(t):
                return [(xres[:, 4 * t + i, :], xbuf[4 * t + i]) for i in range(4)]

            for st in range(2):
                if st == 0:
                    norm_tile(x_blocks(0), 0, hT[0])
                mixer_tile(st, 0, norm_hooks(x_blocks(1), 0, hT[1]), late=(st == 1))
                mixer_out(st, 0)
                mixer_tile(st, 1, norm_hooks(x_blocks(0), 1, hT[0]))
                for nm in ("in_q", "in_f", "in_i", "in_g", "in_p"):
                    release(nm, st)
                mixer_out(st, 1)
                release("out0", st)
                release("out1", st)
                load_gain(0, G_FFN)
                ckpt(f"p1_{st}")
                hkA = norm_hooks(x_blocks(1), 1, hT[1])
                hkA.run_a()
                for th in xq_groups(st, 0):
                    th()
                hkA.run_b()
                weave2(core_steps(st, 0), xq_groups(st, 1))
                weave2(core_steps(st, 1), xo_groups(st, 0))
                hkB = norm_hooks(x_blocks(0), 0, hT[0])
                hkB.run_a()
                xo1 = xo_groups(st, 1)
                for th in xo1[:4]:
                    th()
                hkB.run_b()
                for th in xo1[4:]:
                    th()
                for nm in ("xq0", "xq1", "xo0", "xo1"):
                    release(nm, st)
                load_gain(1, G_FIN)
                ckpt(f"p2_{st}")
                items = [(q, t) for q in range(4) for t in range(2)]
                us = {items[0]: ffn_ff1(st, 0, 0, norm_hooks(x_blocks(1), 0, hT[1]))}
                for i, (q, t) in enumerate(items):
                    if i + 1 < len(items):
                        q2, t2 = items[i + 1]
                        us[(q2, t2)] = ffn_ff1(st, q2, t2)
                    hk = NOHOOK
                    if (q, t) == (3, 1) and st == 0:
                        load_gain(0, G_MIX)
                        hk = norm_hooks(x_blocks(0), 0, hT[0])
                    ffn_ff2(st, q, t, us[(q, t)], hk)
                    if q == 3:
                        final_tile(st, t, 1)
                        if st == 0:
                            for i_ in range(4):
                                load_x_block(x_d, 1024 + 128 * (4 * t + i_), 4 * t + i_)
                    if t == 1:
                        for nm in (f"ff1_{2 * q}", f"ff1_{2 * q + 1}", f"ff2_{q}0", f"ff2_{q}1"):
                            release(nm, st)
                        ckpt(f"f{q}_{st}")
                if st == 0:
                    load_gain(1, G_X)


        except _Stop:
            for b_ in range(8):
                S.dma("sp", y_d[128 * b_:128 * b_ + 128, :], xres[:, b_, :], f"x{b_}", reads=[xbuf[b_]])
                out_sems.add(f"x{b_}")
        stats = S.emit(final_wait_sems=sorted(out_sems))
        build_program.stats = stats
    return nc


_CACHE = {}


def _constants():
    ident = np.eye(128, dtype=np.float32)
    s = np.arange(128)[:, None]
    t = np.arange(128)[None, :]
    m = ((t >= s) & ((s // 64) == (t // 64))).astype(np.float32)
    mask4 = np.tile(m, (1, 4))
    scan = np.ones((128, 512), np.float32)
    scan[:, ::64] = 0.0
    return np.ascontiguousarray(np.concatenate([ident, mask4, scan], axis=1))


def kernel(x, mem, norm_mix_g, w_in, lb_logits, hgrn_norm_g, w_pool, pool_scale, w_out,
           norm_x_g, norm_mem_g, w_xq, w_xk, w_xv, w_xo, norm_ffn_g, w_ff1, w_ff2, final_norm_g):
    f32 = np.float32
    x = np.asarray(x, f32)
    mem = np.asarray(mem, f32)
    if "nc" not in _CACHE:
        _CACHE["nc"] = build_program()
    nc = _CACHE["nc"]
    c = lambda a: np.ascontiguousarray(np.asarray(a, f32))
    gains = c(np.stack([np.asarray(norm_mix_g)[0], np.asarray(norm_x_g)[0], np.asarray(norm_mem_g)[0],
                        np.asarray(norm_ffn_g)[0], np.asarray(final_norm_g)]))
    lbl = c(np.asarray(lb_logits, f32).reshape(2, 4, 128).transpose(2, 0, 1).reshape(128, 8))
    gno = c(np.asarray(hgrn_norm_g, f32)[0].T)
    psc = c(np.asarray(pool_scale, f32)[0].reshape(4, 128).T)
    shared = {
        "w_in": c(np.asarray(w_in)[0]), "w_out": c(np.asarray(w_out)[0]),
        "w_xq": c(np.asarray(w_xq)[0].reshape(D, D)), "w_xk": c(np.asarray(w_xk)[0].reshape(D, D)),
        "w_xv": c(np.asarray(w_xv)[0].reshape(D, D)), "w_xo": c(np.asarray(w_xo)[0].reshape(D, D)),
        "w_ff1": c(np.asarray(w_ff1)[0]), "w_ff2": c(np.asarray(w_ff2)[0]),
        "w_pool": c(np.asarray(w_pool)[0]), "gains": gains, "lbl": lbl, "gno": gno, "psc": psc,
        "cst": _constants(),
    }
    icnt_first = np.zeros((4, 16), f32)
    icnt_rest = np.zeros((4, 16), f32)
    for g, w in enumerate(POOL_W):
        for t in range(16):
            icnt_first[g, t] = 1.0 / min(t + 1, w)
            icnt_rest[g, t] = 1.0 / w
    in_maps = []
    for core in range(8):
        b, half = divmod(core, 2)
        m = dict(shared)
        m["x"] = c(x[b, half * NTOK:(half + 1) * NTOK])
        m["xp"] = c(x[b, 0:NTOK]) if half == 1 else np.zeros((NTOK, D), f32)
        m["mem"] = c(mem[b])
        ic = icnt_first if half == 0 else icnt_rest
        m["icnt"] = c(np.broadcast_to(ic.reshape(1, 64), (128, 64)))
        in_maps.append(m)
    res = run_bass_kernel_spmd(nc, in_maps, core_ids=list(range(8)))
    out = np.empty((4, 4096, D), f32)
    for core in range(8):
        b, half = divmod(core, 2)
        out[b, half * NTOK:(half + 1) * NTOK] = res.results[core]["y"]
    return out
```
